# Optimizing a Trainium2 kernel written in Bass

```python
import math
import jax, jax.numpy as jnp
from jax import lax
import numpy as np

D_MODEL = 4096
BATCH = 2
SEQ = 4096
DEPTH = 2
DEC_BATCH = 1
DEC_SEQ = 8192
PAST_LEN = 128

D_MIX = D_MODEL
SSD_INNER = 3 * D_MIX // 8
SSD_HEADDIM = 64
SSD_HEADS = SSD_INNER // SSD_HEADDIM
SSD_GROUPS = 4
SSD_STATE = 128
SSD_CONV = 5
SSD_CHUNK = 128
CONV_CH = SSD_INNER + 2 * SSD_GROUPS * SSD_STATE
DIFF_WIDTH = D_MIX // 4
DIFF_HEAD_DIM = 128
DIFF_HEADS = DIFF_WIDTH // DIFF_HEAD_DIM
DIFF_QK_DIM = DIFF_HEAD_DIM // 2
DIL_WIDTH = D_MIX - SSD_INNER - DIFF_WIDTH
DIL_HEAD_DIM = 128
DIL_HEADS = DIL_WIDTH // DIL_HEAD_DIM
DIL_WINDOWS = (128, 512, 2048)
DIL_RATES = (1, 4, 16)
ATTN_QBLOCK = 128
PROJ_SIZES = (SSD_INNER, CONV_CH, 2 * SSD_HEADS, DIFF_WIDTH, DIFF_WIDTH, DIFF_WIDTH, DIL_WIDTH, DIL_WIDTH, DIL_WIDTH)
PROJ_WIDTH = sum(PROJ_SIZES)
PEER_HEADS = 8
PEER_NKEYS = 128
PEER_EXPERTS = PEER_NKEYS * PEER_NKEYS
PEER_TOPK = 16
PEER_DKEY = 256
PEER_TOKEN_BLOCK = 64
LN_EPS = 1e-5
RMS_EPS = 1e-5
ALPHA = (2 * DEPTH) ** 0.25
BETA = (8 * DEPTH) ** -0.25

kernel_name = 'hymba_ssd_diff_dilated_peer_encoder'


def layer_norm(x, g, b):
    xf = x.astype(jnp.float32)
    mu = jnp.mean(xf, axis=-1, keepdims=True)
    var = jnp.mean(jnp.square(xf - mu), axis=-1, keepdims=True)
    return ((xf - mu) * lax.rsqrt(var + LN_EPS) * g + b).astype(x.dtype)


def rms_norm(x, g):
    xf = x.astype(jnp.float32)
    return xf * lax.rsqrt(jnp.mean(jnp.square(xf), axis=-1, keepdims=True) + RMS_EPS) * g


def alibi_slopes(n):
    return jnp.exp2(-8.0 * jnp.arange(1, n + 1, dtype=jnp.float32) / n)


def centred_dwconv(x, w, b):
    pad = (w.shape[0] - 1) // 2
    y = lax.conv_general_dilated(x, w[:, None, :].astype(x.dtype), window_strides=(1,),
                                 padding=((pad, pad),), dimension_numbers=('NWC', 'WIO', 'NWC'),
                                 feature_group_count=x.shape[-1])
    return y + b


def ssd_chunked(x, dt, a, b, c):
    bsz, t, nh, p = x.shape
    g, n = b.shape[2], b.shape[3]
    r = nh // g
    q = SSD_CHUNK
    nc = t // q
    dtf = dt.astype(jnp.float32)
    xd = (x.astype(jnp.float32) * dtf[..., None]).reshape(bsz, nc, q, g, r, p)
    log_a = (dtf * a.astype(jnp.float32)).reshape(bsz, nc, q, g, r)
    bc = b.astype(jnp.float32).reshape(bsz, nc, q, g, n)
    cc = c.astype(jnp.float32).reshape(bsz, nc, q, g, n)
    acum = jnp.cumsum(log_a, axis=2)
    seg = acum[:, :, :, None] - acum[:, :, None, :]
    lower = jnp.tril(jnp.ones((q, q), dtype=bool))[None, None, :, :, None, None]
    decay = jnp.exp(jnp.where(lower, seg, -jnp.inf))
    cb = jnp.einsum('bclgn,bcsgn->bclsg', cc, bc)
    y_diag = jnp.einsum('bclsgr,bcsgrp->bclgrp', cb[..., None] * decay, xd)
    end_decay = jnp.exp(acum[:, :, -1:] - acum)
    chunk_states = jnp.einsum('bclgn,bclgr,bclgrp->bcgrpn', bc, end_decay, xd)
    chunk_decay = jnp.exp(acum[:, :, -1])

    def step(h, inp):
        st, dec = inp
        return h * dec[..., None, None] + st, h

    h0 = jnp.zeros((bsz, g, r, p, n), jnp.float32)
    _, h_in = lax.scan(step, h0, (jnp.moveaxis(chunk_states, 1, 0), jnp.moveaxis(chunk_decay, 1, 0)))
    h_in = jnp.moveaxis(h_in, 0, 1)
    y_off = jnp.einsum('bclgn,bcgrpn,bclgr->bclgrp', cc, h_in, jnp.exp(acum))
    return (y_diag + y_off).reshape(bsz, t, nh, p)


def ssd_mixer(z, xbc, dt_raw, conv_w, conv_b, dt_bias, a_log, d_skip, norm_g):
    bsz, t, _ = z.shape
    xbc = jax.nn.silu(centred_dwconv(xbc, conv_w, conv_b))
    xs, b, c = jnp.split(xbc, [SSD_INNER, SSD_INNER + SSD_GROUPS * SSD_STATE], axis=-1)
    xs = xs.reshape(bsz, t, SSD_HEADS, SSD_HEADDIM)
    b = b.reshape(bsz, t, SSD_GROUPS, SSD_STATE)
    c = c.reshape(bsz, t, SSD_GROUPS, SSD_STATE)
    dt = jax.nn.softplus(dt_raw.astype(jnp.float32).reshape(bsz, t, 2, SSD_HEADS) + dt_bias.astype(jnp.float32))
    a = -jnp.exp(a_log.astype(jnp.float32))
    flip = lambda u: jnp.flip(u, axis=1)
    y_fwd = ssd_chunked(xs, dt[:, :, 0], a[0], b, c)
    y_bwd = flip(ssd_chunked(flip(xs), flip(dt[:, :, 1]), a[1], flip(b), flip(c)))
    y = y_fwd + y_bwd + d_skip.astype(jnp.float32)[:, None] * xs.astype(jnp.float32)
    y = y.reshape(bsz, t, SSD_INNER) * jax.nn.silu(z.astype(jnp.float32))
    return rms_norm(y, norm_g).astype(z.dtype)


def diff_attention(q, k, v, lambdas, norm_g, lambda_init):
    bsz, t, _ = q.shape
    q = q.reshape(bsz, t, DIFF_HEADS, 2, DIFF_QK_DIM)
    k = k.reshape(bsz, t, DIFF_HEADS, 2, DIFF_QK_DIM)
    v = v.reshape(bsz, t, DIFF_HEADS, DIFF_HEAD_DIM)
    lf = lambdas.astype(jnp.float32)
    lam = jnp.exp(jnp.sum(lf[0] * lf[1])) - jnp.exp(jnp.sum(lf[2] * lf[3])) + lambda_init
    slopes = alibi_slopes(DIFF_HEADS)
    scale = DIFF_QK_DIM ** -0.5
    nb = t // ATTN_QBLOCK
    qb = jnp.moveaxis(q.reshape(bsz, nb, ATTN_QBLOCK, DIFF_HEADS, 2, DIFF_QK_DIM), 1, 0)
    kpos = jnp.arange(t)

    def block(args):
        qblk, start = args
        s = jnp.einsum('bqhmd,bkhmd->bhmqk', qblk, k).astype(jnp.float32) * scale
        qpos = start + jnp.arange(ATTN_QBLOCK)
        dist = jnp.abs(qpos[:, None] - kpos[None, :]).astype(jnp.float32)
        p = jax.nn.softmax(s - (slopes[:, None, None] * dist)[None, :, None], axis=-1)
        w = p[:, :, 0] - lam * p[:, :, 1]
        return jnp.einsum('bhqk,bkhd->bqhd', w.astype(v.dtype), v)

    o = lax.map(block, (qb, jnp.arange(nb) * ATTN_QBLOCK))
    o = jnp.moveaxis(o, 0, 1).reshape(bsz, t, DIFF_HEADS, DIFF_HEAD_DIM)
    o = rms_norm(o, norm_g) * (1.0 - lambda_init)
    return o.reshape(bsz, t, DIFF_WIDTH).astype(q.dtype)


def dilated_branch(q, k, v, rate, radius, slopes):
    bsz, t, nh, dh = q.shape
    length = t // rate
    nb = -(-length // radius)
    lp = nb * radius

    def residues(u):
        return jnp.moveaxis(u.reshape(bsz, length, rate, nh, dh), 2, 1).reshape(bsz * rate, length, nh, dh)

    qr, kr, vr = residues(q), residues(k), residues(v)
    qb = jnp.pad(qr, ((0, 0), (0, lp - length), (0, 0), (0, 0))).reshape(bsz * rate, nb, radius, nh, dh)
    pad_kv = ((0, 0), (radius, lp - length + radius), (0, 0), (0, 0))

    def neighbours(u):
        ub = jnp.pad(u, pad_kv).reshape(bsz * rate, nb + 2, radius, nh, dh)
        return jnp.concatenate([ub[:, :-2], ub[:, 1:-1], ub[:, 2:]], axis=2)

    kb, vb = neighbours(kr), neighbours(vr)
    s = jnp.einsum('gnqhd,gnkhd->gnhqk', qb, kb).astype(jnp.float32) * (DIL_HEAD_DIM ** -0.5)
    blk = jnp.arange(nb)[:, None, None] * radius
    qpos = blk + jnp.arange(radius)[None, :, None]
    kpos = blk - radius + jnp.arange(3 * radius)[None, None, :]
    rel = jnp.abs(kpos - qpos)
    valid = (rel <= radius) & (kpos >= 0) & (kpos < length)
    bias = -slopes[:, None, None, None] * (rate * rel).astype(jnp.float32)
    s = jnp.where(valid[None, :, None], s + jnp.moveaxis(bias, 0, 1)[None], -jnp.inf)
    m = jnp.max(s, axis=-1, keepdims=True)
    e = jnp.exp(s - m)
    den = jnp.sum(e, axis=-1, keepdims=True)
    o = jnp.einsum('gnhqk,gnkhd->gnqhd', (e / den).astype(v.dtype), vb)
    lse = (m + jnp.log(den))[..., 0]
    o = o.reshape(bsz * rate, lp, nh, dh)[:, :length]
    lse = jnp.moveaxis(lse, 2, 3).reshape(bsz * rate, lp, nh)[:, :length]

    def positions(u):
        return jnp.moveaxis(u.reshape(bsz, rate, length, *u.shape[2:]), 1, 2).reshape(bsz, t, *u.shape[2:])

    return positions(o), positions(lse)


def dilated_attention(q, k, v):
    bsz, t, _ = q.shape
    q = q.reshape(bsz, t, DIL_HEADS, DIL_HEAD_DIM)
    k = k.reshape(bsz, t, DIL_HEADS, DIL_HEAD_DIM)
    v = v.reshape(bsz, t, DIL_HEADS, DIL_HEAD_DIM)
    slopes = alibi_slopes(DIL_HEADS)
    outs, lses = [], []
    for window, rate in zip(DIL_WINDOWS, DIL_RATES):
        o, lse = dilated_branch(q, k, v, rate, window // (2 * rate), slopes)
        outs.append(o.astype(jnp.float32))
        lses.append(lse)
    w = jax.nn.softmax(jnp.stack(lses, axis=0), axis=0)
    o = jnp.sum(w[..., None] * jnp.stack(outs, axis=0), axis=0)
    return o.reshape(bsz, t, DIL_WIDTH).astype(q.dtype)


def token_mixer(h, w_in, conv_w, conv_b, dt_bias, a_log, d_skip, ssd_norm_g, diff_lambda, diff_norm_g, w_out, lambda_init):
    proj = h @ w_in
    idx, acc = [], 0
    for size in PROJ_SIZES[:-1]:
        acc += size
        idx.append(acc)
    z, xbc, dt_raw, dq, dk, dv, lq, lk, lv = jnp.split(proj, idx, axis=-1)
    y_ssd = ssd_mixer(z, xbc, dt_raw, conv_w, conv_b, dt_bias, a_log, d_skip, ssd_norm_g)
    y_diff = diff_attention(dq, dk, dv, diff_lambda, diff_norm_g, lambda_init)
    y_dil = dilated_attention(lq, lk, lv)
    cat = jnp.concatenate([y_ssd, y_diff, y_dil], axis=-1).astype(h.dtype)
    return cat @ w_out


def peer(h, wq, keys, u, v):
    bsz, t, d = h.shape
    xt = h.reshape(-1, PEER_TOKEN_BLOCK, d)

    def block(xb):
        q = (xb @ wq).reshape(PEER_TOKEN_BLOCK, PEER_HEADS, 2, PEER_DKEY // 2)
        s = jnp.einsum('thcd,hckd->thck', q, keys).astype(jnp.float32)
        s1, i1 = lax.top_k(s[:, :, 0], PEER_TOPK)
        s2, i2 = lax.top_k(s[:, :, 1], PEER_TOPK)
        cand = (s1[..., :, None] + s2[..., None, :]).reshape(PEER_TOKEN_BLOCK, PEER_HEADS, PEER_TOPK * PEER_TOPK)
        cand_idx = (i1[..., :, None] * PEER_NKEYS + i2[..., None, :]).reshape(PEER_TOKEN_BLOCK, PEER_HEADS, PEER_TOPK * PEER_TOPK)
        top, pos = lax.top_k(cand, PEER_TOPK)
        eidx = jnp.take_along_axis(cand_idx, pos, axis=-1)
        gate = jax.nn.softmax(top, axis=-1)
        u_e = jnp.take(u, eidx, axis=0)
        act = jax.nn.gelu(jnp.einsum('thkd,td->thk', u_e, xb).astype(jnp.float32), approximate=False)
        v_e = jnp.take(v, eidx, axis=0)
        return jnp.einsum('thk,thkd->td', (gate * act).astype(v.dtype), v_e)

    out = lax.map(block, xt)
    return out.reshape(bsz, t, d)


def trunk(x, ln_in_g, ln_in_b, w_in, conv_w, conv_b, dt_bias, a_log, d_skip, ssd_norm_g, diff_lambda,
          diff_norm_g, w_out, ln1_g, ln1_b, peer_wq, peer_keys, peer_u, peer_v, ln2_g, ln2_b):
    h = layer_norm(x, ln_in_g, ln_in_b)
    for l in range(DEPTH):
        lambda_init = 0.8 - 0.6 * math.exp(-0.3 * l)
        mix = token_mixer(h, w_in[l], conv_w[l], conv_b[l], dt_bias[l], a_log[l], d_skip[l], ssd_norm_g[l],
                          diff_lambda[l], diff_norm_g[l], w_out[l], lambda_init)
        h = layer_norm(ALPHA * h + mix, ln1_g[l], ln1_b[l])
        h = layer_norm(ALPHA * h + peer(h, peer_wq[l], peer_keys[l], peer_u[l], peer_v[l]), ln2_g[l], ln2_b[l])
    return h


def setup_inputs(seed: int = 0) -> dict:
    key = jax.random.key(seed)
    ks = jax.random.split(key, 24)
    f32 = jnp.float32

    def nrm(k, shape, scale):
        return jax.random.normal(k, shape, f32) * scale

    def gain(k, shape):
        return 1.0 + 0.01 * jax.random.normal(k, shape, f32)

    dt0 = jnp.exp(jax.random.uniform(ks[7], (DEPTH, 2, SSD_HEADS), f32, math.log(1e-3), math.log(1e-1)))
    return {
        'x_prompt': nrm(ks[0], (BATCH, SEQ, D_MODEL), 1.0),
        'x_sample': nrm(ks[1], (DEC_BATCH, DEC_SEQ, D_MODEL), 1.0),
        'ln_in_g': gain(ks[2], (D_MODEL,)),
        'ln_in_b': nrm(ks[3], (D_MODEL,), 0.01),
        'w_in': nrm(ks[4], (DEPTH, D_MODEL, PROJ_WIDTH), D_MODEL ** -0.5),
        'conv_w': nrm(ks[5], (DEPTH, SSD_CONV, CONV_CH), SSD_CONV ** -0.5),
        'conv_b': nrm(ks[6], (DEPTH, CONV_CH), 0.01),
        'dt_bias': dt0 + jnp.log(-jnp.expm1(-dt0)),
        'a_log': jnp.log(jax.random.uniform(ks[8], (DEPTH, 2, SSD_HEADS), f32, 1.0, 16.0)),
        'd_skip': gain(ks[9], (DEPTH, SSD_HEADS)),
        'ssd_norm_g': gain(ks[10], (DEPTH, SSD_INNER)),
        'diff_lambda': nrm(ks[11], (DEPTH, 4, DIFF_QK_DIM), 0.1),
        'diff_norm_g': gain(ks[12], (DEPTH, DIFF_HEAD_DIM)),
        'w_out': nrm(ks[13], (DEPTH, D_MIX, D_MODEL), BETA * D_MIX ** -0.5),
        'ln1_g': gain(ks[14], (DEPTH, D_MODEL)),
        'ln1_b': nrm(ks[15], (DEPTH, D_MODEL), 0.01),
        'peer_wq': nrm(ks[16], (DEPTH, D_MODEL, PEER_HEADS * PEER_DKEY), D_MODEL ** -0.5),
        'peer_keys': nrm(ks[17], (DEPTH, PEER_HEADS, 2, PEER_NKEYS, PEER_DKEY // 2), (PEER_DKEY // 2) ** -0.5),
        'peer_u': nrm(ks[18], (DEPTH, PEER_EXPERTS, D_MODEL), D_MODEL ** -0.5),
        'peer_v': nrm(ks[19], (DEPTH, PEER_EXPERTS, D_MODEL), BETA * PEER_HEADS ** -0.5),
        'ln2_g': gain(ks[20], (DEPTH, D_MODEL)),
        'ln2_b': nrm(ks[21], (DEPTH, D_MODEL), 0.01),
    }


def reference(x_prompt, x_sample, ln_in_g, ln_in_b, w_in, conv_w, conv_b, dt_bias, a_log, d_skip, ssd_norm_g,
              diff_lambda, diff_norm_g, w_out, ln1_g, ln1_b, peer_wq, peer_keys, peer_u, peer_v, ln2_g, ln2_b):
    y_prompt = trunk(x_prompt, ln_in_g, ln_in_b, w_in, conv_w, conv_b, dt_bias, a_log, d_skip, ssd_norm_g,
                     diff_lambda, diff_norm_g, w_out, ln1_g, ln1_b, peer_wq, peer_keys, peer_u, peer_v, ln2_g, ln2_b)
    y_sample = trunk(x_sample, ln_in_g, ln_in_b, w_in, conv_w, conv_b, dt_bias, a_log, d_skip, ssd_norm_g,
                     diff_lambda, diff_norm_g, w_out, ln1_g, ln1_b, peer_wq, peer_keys, peer_u, peer_v, ln2_g, ln2_b)
    return (y_prompt, y_sample)
```

```python
import contextlib
import numpy as np
import concourse.bass as bass
import concourse.mybir as mybir
from concourse.bass import ds
from concourse.bass_utils import run_bass_kernel_spmd

F32 = mybir.dt.float32
BF16 = mybir.dt.bfloat16
I32 = mybir.dt.int32
AF = mybir.ActivationFunctionType
ALU = mybir.AluOpType
AX = mybir.AxisListType

NCORE = 2
D = 4096
DEPTH = 2
TL = 8192
NT = TL // 128
NCH = TL // 128
HALF = NCH // 2
PW = 11824
OFF_Z, OFF_XBC, OFF_DT, OFF_DQ, OFF_DK, OFF_DV, OFF_LQ, OFF_LK, OFF_LV = 0, 1536, 4096, 4144, 5168, 6192, 7216, 8752, 10288
ALPHA = (2 * DEPTH) ** 0.25
LN_EPS = 1e-5
SAME_ENGINE_SYNC = True


class Buf:
    __slots__ = ("w", "r")

    def __init__(self):
        self.w = None
        self.r = {}


class Sched:
    def __init__(self, nc):
        self.nc = nc
        self.E = dict(pe=nc.tensor, dve=nc.vector, act=nc.scalar, pool=nc.gpsimd, sp=nc.sync)
        self.cnt = {e: 0 for e in self.E}
        self.sem = {e: nc.alloc_semaphore(name=f"sem_{e}") for e in self.E}
        self.rings = {}
        for q, n in (("sp", 12), ("pool", 8), ("act", 4)):
            self.rings[q] = dict(sems=[nc.alloc_semaphore(name=f"dsem_{q}{i}") for i in range(n)],
                                 val=[0] * n, tok=[None] * n, nxt=0)
        self.seen = {e: {} for e in self.E}
        self.semof = {("c", e): s for e, s in self.sem.items()}
        for q, rg in self.rings.items():
            for i, s in enumerate(rg["sems"]):
                self.semof[("d", q, i)] = s

    def _wait(self, e, toks):
        need = {}
        for t in toks:
            if t is None:
                continue
            k, v = t
            if need.get(k, 0) < v:
                need[k] = v
        for k, v in need.items():
            if k == ("c", e) and (not SAME_ENGINE_SYNC or e in ("pe", "sp")):
                continue
            if self.seen[e].get(k, 0) >= v:
                continue
            self.seen[e][k] = v
            self.E[e].wait_ge(self.semof[k], v)

    @staticmethod
    def _deps(r, w):
        toks = [b.w for b in r] + [b.w for b in w]
        for b in w:
            toks.extend(b.r.items())
        return toks

    @staticmethod
    def _mark(tok, r, w):
        k, v = tok
        for b in r:
            if b.r.get(k, 0) < v:
                b.r[k] = v
        for b in w:
            b.w = tok
            b.r = {}

    def op(self, e, fn, r=(), w=()):
        self._wait(e, self._deps(r, w))
        ins = fn(self.E[e])
        self.cnt[e] += 1
        ins.then_inc(self.sem[e], 1)
        self._mark((("c", e), self.cnt[e]), r, w)

    def dma(self, q, out, in_, r=(), w=(), **kw):
        rg = self.rings[q]
        i = rg["nxt"]
        rg["nxt"] = (i + 1) % len(rg["sems"])
        self._wait(q, self._deps(r, w) + [rg["tok"][i]])
        rg["val"][i] += 16
        self.E[q].dma_start(out=out, in_=in_, **kw).then_inc(rg["sems"][i], 16)
        tok = (("d", q, i), rg["val"][i])
        rg["tok"][i] = tok
        self._mark(tok, r, w)

    def collective(self, kind, alu, ins, outs, r=(), w=()):
        rg = self.rings["pool"]
        i = rg["nxt"]
        rg["nxt"] = (i + 1) % len(rg["sems"])
        self._wait("pool", self._deps(r, w) + [rg["tok"][i]])
        rg["val"][i] += 1
        self.nc.gpsimd.collective_compute(kind, alu, replica_groups=[list(range(NCORE))], ins=ins, outs=outs
                                          ).then_inc(rg["sems"][i], 1)
        tok = (("d", "pool", i), rg["val"][i])
        rg["tok"][i] = tok
        self._mark(tok, r, w)

    def frontier(self):
        toks = [(("c", e), c) for e, c in self.cnt.items() if c > 0]
        for q, rg in self.rings.items():
            toks.extend(t for t in rg["tok"] if t is not None)
        return toks

    def barrier(self):
        toks = self.frontier()
        for e in self.E:
            self._wait(e, [t for t in toks if t[0] != ("c", e)])


class Ctx:
    pass


_uid = [0]


def U(name):
    _uid[0] += 1
    return f"{name}_{_uid[0]}"


def _dram(nc, dbg, name, shape, dtype):
    kind = "ExternalOutput" if name in dbg else "Internal"
    return nc.dram_tensor(name, list(shape), dtype, kind=kind).ap()


def ln_phase(C, S, src_fn, g_ap, b_ap, write_out=None):
    nc = C.nc
    with (nc.sbuf_tensor(U("ln_g"), [128, D], F32) as gt, nc.sbuf_tensor(U("ln_b"), [128, D], F32) as bt,
          nc.sbuf_tensor(U("ln_x0"), [128, D], F32) as x0, nc.sbuf_tensor(U("ln_x1"), [128, D], F32) as x1,
          nc.sbuf_tensor(U("ln_y0"), [128, D], BF16) as y0, nc.sbuf_tensor(U("ln_y1"), [128, D], BF16) as y1,
          nc.sbuf_tensor(U("ln_t0"), [128, 32, 128], BF16) as t0, nc.sbuf_tensor(U("ln_t1"), [128, 32, 128], BF16) as t1,
          nc.sbuf_tensor(U("ln_st"), [128, 8, 6], F32) as st, nc.sbuf_tensor(U("ln_mv"), [128, 4], F32) as mv,
          nc.psum_tensor(U("ln_p0"), [128, 8, 128], BF16) as p0, nc.psum_tensor(U("ln_p1"), [128, 8, 128], BF16) as p1):
        bg, bb, bst, bmv = Buf(), Buf(), Buf(), Buf()
        xs, ys, ts, ps = [x0, x1], [y0, y1], [t0, t1], [p0, p1]
        bx, by, bt_, bp = [Buf(), Buf()], [Buf(), Buf()], [Buf(), Buf()], [Buf(), Buf()]
        S.dma("sp", gt[:], g_ap.partition_broadcast(128), w=[bg])
        S.dma("sp", bt[:], b_ap.partition_broadcast(128), w=[bb])
        for t in range(NT):
            k = t % 2
            x, y, tt = xs[k], ys[k], ts[k]
            src_fn(t, x, bx[k])
            for c in range(8):
                S.op("dve", lambda e, c=c, x=x: e.bn_stats(st[:, c, :], x[:, c * 512:(c + 1) * 512]), r=[bx[k]], w=[bst])
            S.op("dve", lambda e: e.bn_aggr(mv[:, 0:2], st[:].rearrange("p a b -> p (a b)")), r=[bst], w=[bmv])
            S.op("act", lambda e: e.activation(mv[:, 3:4], mv[:, 1:2], AF.Sqrt, bias=C.eps_t[:, 0:1], scale=1.0), r=[bmv, C.b_ident], w=[bmv])
            S.op("dve", lambda e: e.reciprocal(mv[:, 2:3], mv[:, 3:4]), r=[bmv], w=[bmv])
            S.op("dve", lambda e, x=x: e.tensor_scalar(x[:], x[:], mv[:, 0:1], mv[:, 2:3], ALU.subtract, ALU.mult),
                 r=[bmv, bx[k]], w=[bx[k]])
            S.op("pool", lambda e, x=x: e.tensor_tensor(x[:], x[:], gt[:], ALU.mult), r=[bg, bx[k]], w=[bx[k]])
            S.op("pool", lambda e, x=x: e.tensor_tensor(x[:], x[:], bt[:], ALU.add), r=[bb, bx[k]], w=[bx[k]])
            S.dma("sp", C.hres[t * 128:(t + 1) * 128, :], x[:], r=[bx[k]], w=[C.b_hres])
            if write_out is not None:
                write_out(t, x, bx[k])
            S.op("act", lambda e, x=x, y=y: e.copy(y[:], x[:]), r=[bx[k]], w=[by[k]])
            for g in range(4):
                pk = ps[g % 2]
                for j in range(8):
                    c = g * 8 + j
                    S.op("pe", lambda e, pk=pk, j=j, c=c, y=y: e.transpose(pk[:, j, :], y[:, c * 128:(c + 1) * 128], C.ident[:]),
                         r=[by[k], C.b_ident], w=[bp[g % 2]])
                eng = "dve" if g % 2 == 0 else "act"
                if eng == "dve":
                    S.op("dve", lambda e, pk=pk, g=g, tt=tt: e.tensor_copy(tt[:, g * 8:(g + 1) * 8, :], pk[:]), r=[bp[g % 2]], w=[bt_[k]])
                else:
                    S.op("act", lambda e, pk=pk, g=g, tt=tt: e.copy(tt[:, g * 8:(g + 1) * 8, :], pk[:]), r=[bp[g % 2]], w=[bt_[k]])
            S.dma("sp", C.hT.rearrange("(c p) t -> p c t", p=128)[:, :, t * 128:(t + 1) * 128], tt[:], r=[bt_[k]], w=[C.b_hT])
        S.barrier()


def proj_blocks(C, l):
    blocksA = []
    for j in range(3):
        blocksA.append((OFF_Z + j * 512, 512, "f32", lambda t0, j=j: C.z_d[t0:t0 + 128, j * 512:(j + 1) * 512]))
    for j in range(5):
        blocksA.append((OFF_XBC + j * 512, 512, "f32", lambda t0, j=j: C.xbc_d[t0:t0 + 128, j * 512:(j + 1) * 512]))
    blocksA.append((OFF_DT, 48, "f32", lambda t0: C.dt_d[t0:t0 + 128, :]))
    for j in range(2):
        blocksA.append((OFF_DV + j * 512, 512, "v", lambda t0, j=j: C.Vd[t0:t0 + 128, j * 4:(j + 1) * 4, :]))
    for j in range(3):
        blocksA.append((OFF_LV + j * 512, 512, "v", lambda t0, j=j: C.Vl[t0:t0 + 128, j * 4:(j + 1) * 4, :]))
    blocksB = []
    for j in range(2):
        blocksB.append((OFF_DQ + j * 512, 0.125, lambda ch, tb, j=j: C.QTd[j * 512 + ch * 128:j * 512 + (ch + 1) * 128, tb:tb + 512]))
    for j in range(2):
        blocksB.append((OFF_DK + j * 512, 1.0, lambda ch, tb, j=j: C.KTd[j * 512 + ch * 128:j * 512 + (ch + 1) * 128, tb:tb + 512]))
    for j in range(3):
        blocksB.append((OFF_LQ + j * 512, 128 ** -0.5, lambda ch, tb, j=j: C.QTl[j * 512 + ch * 128:j * 512 + (ch + 1) * 128, tb:tb + 512]))
    for j in range(3):
        blocksB.append((OFF_LK + j * 512, 1.0, lambda ch, tb, j=j: C.KTl[j * 512 + ch * 128:j * 512 + (ch + 1) * 128, tb:tb + 512]))
    return blocksA, blocksB


def gemm_phase(C, S, srcT, b_src, w, blocksA, blocksB):
    nc = C.nc
    wv = w.rearrange("(c p) j -> p c j", p=128)
    wqueue = "sp" if w.dtype == BF16 else "pool"
    with (nc.sbuf_tensor(U("pj_h"), [128, 32, 1024], BF16) as hs,
          nc.sbuf_tensor(U("pj_w0"), [128, 32, 512], BF16) as w0, nc.sbuf_tensor(U("pj_w1"), [128, 32, 512], BF16) as w1,
          nc.sbuf_tensor(U("pj_of"), [128, 4, 512], F32) as of, nc.sbuf_tensor(U("pj_ob"), [128, 4, 512], BF16) as ob,
          nc.sbuf_tensor(U("pj_ov"), [128, 4, 4, 129], BF16) as ov,
          nc.psum_tensor(U("pj_ps"), [128, 4, 512], F32) as ps):
        bh = Buf()
        wts, bw = [w0, w1], [Buf(), Buf()]
        bof, bob, bov, bps = [Buf() for _ in range(4)], [Buf() for _ in range(4)], [Buf() for _ in range(4)], [Buf() for _ in range(4)]
        S.op("pool", lambda e: e.memset(ov[:], 1.0), w=bov)
        nblk = 0
        nev = 0
        hv = srcT.rearrange("(c p) t -> p c t", p=128)
        for tb in range(TL // 1024):
            for q in range(4):
                S.dma("sp", hs[:, q * 8:(q + 1) * 8, :], hv[:, q * 8:(q + 1) * 8, tb * 1024:(tb + 1) * 1024], r=[b_src], w=[bh])
            for (col0, ncols, kind, dest) in blocksA:
                wk = nblk % 2
                nblk += 1
                wt = wts[wk]
                for q in range(2):
                    S.dma(wqueue, wt[:, q * 16:(q + 1) * 16, :ncols], wv[:, q * 16:(q + 1) * 16, col0:col0 + ncols], w=[bw[wk]])
                for ti in range(8):
                    pk = nev % 4
                    nev += 1
                    for c in range(32):
                        S.op("pe", lambda e, pk=pk, c=c, ti=ti, wt=wt, ncols=ncols: e.matmul(
                            ps[:, pk, :ncols], hs[:, c, ti * 128:(ti + 1) * 128], wt[:, c, :ncols], start=(c == 0), stop=(c == 31)),
                            r=[bh, bw[wk]], w=[bps[pk]])
                    t0 = tb * 1024 + ti * 128
                    if kind == "res":
                        S.dma("sp", of[:, pk, :], C.hres[t0:t0 + 128, col0:col0 + 512], r=[C.b_hres], w=[bof[pk]])
                        S.op("dve", lambda e, pk=pk: e.scalar_tensor_tensor(of[:, pk, :], of[:, pk, :], ALPHA, ps[:, pk, :], ALU.mult, ALU.add),
                             r=[bps[pk], bof[pk]], w=[bof[pk]])
                        S.dma("sp", dest(t0), of[:, pk, :], r=[bof[pk]], w=[C.b_pre])
                    elif kind == "f32":
                        if nev % 2 == 0:
                            S.op("act", lambda e, pk=pk, ncols=ncols: e.copy(of[:, pk, :ncols], ps[:, pk, :ncols]), r=[bps[pk]], w=[bof[pk]])
                        else:
                            S.op("dve", lambda e, pk=pk, ncols=ncols: e.tensor_copy(of[:, pk, :ncols], ps[:, pk, :ncols]), r=[bps[pk]], w=[bof[pk]])
                        S.dma("sp", dest(t0), of[:, pk, :ncols], r=[bof[pk]], w=[C.b_proj])
                    else:
                        if nev % 2 == 0:
                            S.op("act", lambda e, pk=pk: e.copy(ov[:, pk, :, 0:128], ps[:, pk, :].rearrange("p (h d) -> p h d", h=4)),
                                 r=[bps[pk]], w=[bov[pk]])
                        else:
                            S.op("dve", lambda e, pk=pk: e.tensor_copy(ov[:, pk, :, 0:128], ps[:, pk, :].rearrange("p (h d) -> p h d", h=4)),
                                 r=[bps[pk]], w=[bov[pk]])
                        S.dma("sp", dest(t0), ov[:, pk, :, :], r=[bov[pk]], w=[C.b_proj])
            for (col0, scale, dest) in blocksB:
                wk = nblk % 2
                nblk += 1
                wt = wts[wk]
                for q in range(2):
                    S.dma(wqueue, wt[:, q * 16:(q + 1) * 16, :], wv[:, q * 16:(q + 1) * 16, col0:col0 + 512], w=[bw[wk]])
                for ch in range(4):
                    for tq in range(2):
                        pk = nev % 4
                        nev += 1
                        for c in range(32):
                            S.op("pe", lambda e, pk=pk, c=c, ch=ch, tq=tq, wt=wt: e.matmul(
                                ps[:, pk, :], wt[:, c, ch * 128:(ch + 1) * 128], hs[:, c, tq * 512:(tq + 1) * 512], start=(c == 0), stop=(c == 31)),
                                r=[bh, bw[wk]], w=[bps[pk]])
                        if nev % 2 == 0:
                            S.op("act", lambda e, pk=pk, scale=scale: e.mul(ob[:, pk, :], ps[:, pk, :], scale), r=[bps[pk]], w=[bob[pk]])
                        else:
                            S.op("dve", lambda e, pk=pk, scale=scale: e.tensor_scalar(ob[:, pk, :], ps[:, pk, :], scale, None, ALU.mult),
                                 r=[bps[pk]], w=[bob[pk]])
                        S.dma("sp", dest(ch, tb * 1024 + tq * 512), ob[:, pk, :], r=[bob[pk]], w=[C.b_proj])
        S.barrier()


def transpose_phase(C, S, src, b_src, dstT, b_dst):
    nc = C.nc
    with (nc.sbuf_tensor(U("tp_y"), [128, 2, D], BF16) as y, nc.sbuf_tensor(U("tp_t"), [128, 2, 32, 128], BF16) as tt,
          nc.psum_tensor(U("tp_p"), [128, 2, 8, 128], BF16) as pp):
        by, bt, bp = [Buf(), Buf()], [Buf(), Buf()], [Buf(), Buf()]
        n = 0
        for t in range(NCH):
            k = t % 2
            S.dma("sp", y[:, k, :], src[t * 128:(t + 1) * 128, :], r=[b_src], w=[by[k]])
            for g in range(4):
                pk = n % 2
                n += 1
                for j in range(8):
                    c = g * 8 + j
                    S.op("pe", lambda e, pk=pk, j=j, c=c, k=k: e.transpose(pp[:, pk, j, :], y[:, k, c * 128:(c + 1) * 128], C.ident[:]),
                         r=[by[k], C.b_ident], w=[bp[pk]])
                if g % 2 == 0:
                    S.op("dve", lambda e, pk=pk, g=g, k=k: e.tensor_copy(tt[:, k, g * 8:(g + 1) * 8, :], pp[:, pk, :, :]), r=[bp[pk]], w=[bt[k]])
                else:
                    S.op("act", lambda e, pk=pk, g=g, k=k: e.copy(tt[:, k, g * 8:(g + 1) * 8, :], pp[:, pk, :, :]), r=[bp[pk]], w=[bt[k]])
            S.dma("sp", dstT.rearrange("(c p) t -> p c t", p=128)[:, :, t * 128:(t + 1) * 128], tt[:, k, :, :], r=[bt[k]], w=[b_dst])
        S.barrier()

def alibi(n):
    return [2.0 ** (-8.0 * (i + 1) / n) for i in range(n)]


def lam_prep(C, S, l, lam_init):
    nc = C.nc
    with nc.sbuf_tensor(U("lp_l"), [128, 4, 64], F32) as lt, nc.sbuf_tensor(U("lp_s"), [128, 4], F32) as ls:
        b = Buf()
        S.dma("sp", lt[:].rearrange("p a b -> p (a b)"), C.diff_lambda[l].rearrange("a b -> (a b)").partition_broadcast(128), w=[b])
        S.dma("sp", C.gn[:], C.diff_norm_g[l].partition_broadcast(128), w=[C.b_gn])
        S.op("dve", lambda e: e.tensor_tensor(lt[:, 0, :], lt[:, 0, :], lt[:, 1, :], ALU.mult), r=[b], w=[b])
        S.op("dve", lambda e: e.tensor_tensor(lt[:, 2, :], lt[:, 2, :], lt[:, 3, :], ALU.mult), r=[b], w=[b])
        S.op("dve", lambda e: e.reduce_sum(ls[:, 0:1], lt[:, 0, :], AX.X), r=[b], w=[b])
        S.op("dve", lambda e: e.reduce_sum(ls[:, 1:2], lt[:, 2, :], AX.X), r=[b], w=[b])
        S.op("act", lambda e: e.activation(ls[:, 2:4], ls[:, 0:2], AF.Exp), r=[b], w=[b])
        S.op("dve", lambda e: e.tensor_tensor(C.nlam[:, 0:1], ls[:, 3:4], ls[:, 2:3], ALU.subtract), r=[b], w=[C.b_gn])
        S.op("dve", lambda e: e.tensor_scalar(C.nlam[:, 0:1], C.nlam[:, 0:1], -lam_init, None, ALU.add), r=[C.b_gn], w=[C.b_gn])
        S.op("dve", lambda e: e.tensor_scalar(C.gn[:], C.gn[:], 1.0 - lam_init, None, ALU.mult), r=[C.b_gn], w=[C.b_gn])
        S.barrier()


def attn_phase(C, S, kind):
    nc = C.nc
    diff = kind == "diff"
    NH = 8 if diff else 12
    slopes = alibi(NH)
    QT, KT, V = (C.QTd, C.KTd, C.Vd) if diff else (C.QTl, C.KTl, C.Vl)
    col0 = 1536 if diff else 2560
    Vv = V.rearrange("(c p) h x -> p c h x", p=128)
    with (nc.sbuf_tensor(U("at_k0"), [128, TL], BF16) as k0, nc.sbuf_tensor(U("at_k1"), [128, TL], BF16) as k1,
          nc.sbuf_tensor(U("at_v0"), [128, NCH, 129], BF16) as v0, nc.sbuf_tensor(U("at_v1"), [128, NCH, 129], BF16) as v1,
          nc.sbuf_tensor(U("at_tab"), [128, 127 if diff else 17, 128], F32) as tab,
          nc.sbuf_tensor(U("at_tb2"), [128, 17, 128], F32) as tb2,
          nc.sbuf_tensor(U("at_q"), [128, 2, 128], BF16) as qs,
          nc.sbuf_tensor(U("at_s"), [128, 2, 512], F32) as sb, nc.sbuf_tensor(U("at_p"), [128, 2, 512], BF16) as pb,
          nc.sbuf_tensor(U("at_o"), [128, 2, 128], F32) as ob, nc.sbuf_tensor(U("at_o2"), [128, 2, 128], F32) as o2,
          nc.sbuf_tensor(U("at_ob"), [128, 2, 128], BF16) as obf, nc.sbuf_tensor(U("at_sm"), [128, 2, 8], F32) as sm,
          nc.psum_tensor(U("at_ps"), [128, 2, 512], F32) as ps, nc.psum_tensor(U("at_acc"), [128, 4, 512], F32) as acc):
        bk, bv, bq = [Buf(), Buf()], [Buf(), Buf()], [Buf(), Buf()]
        btab, bsb, bpb, bps = Buf(), [Buf(), Buf()], [Buf(), Buf()], [Buf(), Buf()]
        bacc, bo, bobf, bsm = [Buf() for _ in range(4)], [Buf(), Buf()], [Buf(), Buf()], [Buf(), Buf()]
        ks, vs = [k0, k1], [v0, v1]
        if diff:
            for q in range(4):
                S.dma("sp", tab[:, q * 32:min(127, (q + 1) * 32), :], C.tdist[:, q * 32:min(127, (q + 1) * 32), :], w=[btab])
        else:
            S.dma("sp", tb2[:], C.tlnm[:, :, :], w=[btab])
        it = 0
        nq = 0
        for h in range(NH):
            kk = h % 2
            S.dma("sp", ks[kk][:, :], KT[h * 128:(h + 1) * 128, :], r=[C.b_proj], w=[bk[kk]])
            for q in range(4):
                S.dma("sp", vs[kk][:, q * 16:(q + 1) * 16, :], Vv[:, q * 16:(q + 1) * 16, h, :], r=[C.b_proj], w=[bv[kk]])
            if not diff:
                S.dma("sp", tab[:], C.tdw[:, :, :], w=[btab])
                S.op("dve", lambda e, h=h: e.scalar_tensor_tensor(tab[:], tab[:], -slopes[h], tb2[:], ALU.mult, ALU.add), r=[btab], w=[btab])
            for qt in range(NCH):
                qk = nq % 2
                nq += 1
                S.dma("sp", qs[:, qk, :], QT[h * 128:(h + 1) * 128, qt * 128:(qt + 1) * 128], r=[C.b_proj], w=[bq[qk]])
                if diff:
                    chunks = list(range(NCH))
                else:
                    chunks = [c for c in range(qt - 8, qt + 9) if 0 <= c < NCH]
                groups = []
                for part in ([c for c in chunks if c < HALF], [c for c in chunks if c >= HALF]):
                    groups += [part[i:i + 4] for i in range(0, len(part), 4)]
                maps = (0, 1) if diff else (0,)
                for gi, grp in enumerate(groups):
                    for m in maps:
                        pk = it % 2
                        it += 1
                        ak = qk * 2 + m
                        for j, c in enumerate(grp):
                            if diff:
                                S.op("pe", lambda e, pk=pk, j=j, c=c, m=m, kk=kk, qk=qk: e.matmul(
                                    ps[:, pk, j * 128:(j + 1) * 128], ks[kk][m * 64:(m + 1) * 64, c * 128:(c + 1) * 128],
                                    qs[m * 64:(m + 1) * 64, qk, :], start=True, stop=True), r=[bk[kk], bq[qk]], w=[bps[pk]])
                            else:
                                S.op("pe", lambda e, pk=pk, j=j, c=c, kk=kk, qk=qk: e.matmul(
                                    ps[:, pk, j * 128:(j + 1) * 128], ks[kk][:, c * 128:(c + 1) * 128],
                                    qs[:, qk, :], start=True, stop=True), r=[bk[kk], bq[qk]], w=[bps[pk]])
                        n = len(grp) * 128
                        o0 = grp[0] - qt
                        if diff:
                            S.op("dve", lambda e, pk=pk, n=n, o0=o0, h=h: e.scalar_tensor_tensor(
                                sb[:, pk, :n], tab[:, o0 + 63:o0 + 63 + n // 128, :].rearrange("p a b -> p (a b)"), -slopes[h], ps[:, pk, :n],
                                ALU.mult, ALU.add), r=[btab, bps[pk]], w=[bsb[pk]])
                        else:
                            S.op("dve", lambda e, pk=pk, n=n, o0=o0: e.tensor_tensor(
                                sb[:, pk, :n], ps[:, pk, :n], tab[:, o0 + 8:o0 + 8 + n // 128, :].rearrange("p a b -> p (a b)"), ALU.add),
                                r=[btab, bps[pk]], w=[bsb[pk]])
                        cross = (grp[0] < HALF) != (qt < HALF)
                        bias_ap = C.fb[:, 0:1] if cross else C.fb[:, 2:3]
                        S.op("act", lambda e, pk=pk, n=n, bias_ap=bias_ap: e.activation(pb[:, pk, :n], sb[:, pk, :n], AF.Exp, bias=bias_ap, scale=1.0),
                             r=[bsb[pk], C.b_fb], w=[bpb[pk]])
                        for j, c in enumerate(grp):
                            S.op("pe", lambda e, pk=pk, j=j, c=c, ak=ak, kk=kk, first=(gi == 0 and j == 0), last=(gi == len(groups) - 1 and j == len(grp) - 1):
                                 e.matmul(acc[:, ak, 0:129], pb[:, pk, j * 128:(j + 1) * 128], vs[kk][:, c, :], start=first, stop=last),
                                 r=[bpb[pk], bv[kk]], w=[bacc[ak]])
                a0 = qk * 2
                S.op("dve", lambda e, a0=a0, qk=qk: e.reciprocal(sm[:, qk, 0:1], acc[:, a0, 128:129]), r=[bacc[a0]], w=[bsm[qk]])
                if diff:
                    S.op("dve", lambda e, a0=a0, qk=qk: e.reciprocal(sm[:, qk, 1:2], acc[:, a0 + 1, 128:129]), r=[bacc[a0 + 1]], w=[bsm[qk]])
                    S.op("dve", lambda e, qk=qk: e.tensor_tensor(sm[:, qk, 1:2], sm[:, qk, 1:2], C.nlam[:, 0:1], ALU.mult), r=[bsm[qk], C.b_gn], w=[bsm[qk]])
                    S.op("dve", lambda e, a0=a0, qk=qk: e.tensor_scalar(ob[:, qk, :], acc[:, a0, 0:128], sm[:, qk, 0:1], None, ALU.mult),
                         r=[bacc[a0], bsm[qk]], w=[bo[qk]])
                    S.op("dve", lambda e, a0=a0, qk=qk: e.scalar_tensor_tensor(ob[:, qk, :], acc[:, a0 + 1, 0:128], sm[:, qk, 1:2], ob[:, qk, :], ALU.mult, ALU.add),
                         r=[bacc[a0 + 1], bsm[qk], bo[qk]], w=[bo[qk]])
                    S.op("dve", lambda e, qk=qk: e.tensor_tensor(o2[:, qk, :], ob[:, qk, :], ob[:, qk, :], ALU.mult), r=[bo[qk]], w=[bobf[qk]])
                    S.op("dve", lambda e, qk=qk: e.reduce_sum(sm[:, qk, 2:3], o2[:, qk, :], AX.X), r=[bobf[qk]], w=[bsm[qk]])
                    S.op("act", lambda e, qk=qk: e.activation(sm[:, qk, 3:4], sm[:, qk, 2:3], AF.Ln, bias=C.eps_t[:, 1:2], scale=1.0 / 128), r=[bsm[qk], C.b_ident], w=[bsm[qk]])
                    S.op("act", lambda e, qk=qk: e.activation(sm[:, qk, 4:5], sm[:, qk, 3:4], AF.Exp, scale=-0.5), r=[bsm[qk]], w=[bsm[qk]])
                    S.op("dve", lambda e, qk=qk: e.scalar_tensor_tensor(obf[:, qk, :], ob[:, qk, :], sm[:, qk, 4:5], C.gn[:], ALU.mult, ALU.mult),
                         r=[bo[qk], bsm[qk], C.b_gn], w=[bobf[qk]])
                else:
                    S.op("dve", lambda e, a0=a0, qk=qk: e.tensor_scalar(obf[:, qk, :], acc[:, a0, 0:128], sm[:, qk, 0:1], None, ALU.mult),
                         r=[bacc[a0], bsm[qk]], w=[bobf[qk]])
                S.dma("sp", C.cat[qt * 128:(qt + 1) * 128, col0 + h * 128:col0 + (h + 1) * 128], obf[:, qk, :], r=[bobf[qk]], w=[C.b_cat])
        S.barrier()


def bc_mid(ap2, n):
    return ap2.unsqueeze(2).to_broadcast([ap2.shape[0], ap2.shape[1], n])


def bc_grp(ap2, n):
    return ap2.unsqueeze(1).to_broadcast([ap2.shape[0], n, ap2.shape[1]])


def ssd_phase(C, S, l):
    nc = C.nc
    with (nc.sbuf_tensor(U("sa_w"), [128, 5, 2560], F32) as cw, nc.sbuf_tensor(U("sa_b"), [128, 2560], F32) as cbias,
          nc.sbuf_tensor(U("sa_x"), [128, 2, 5, 1280], F32) as xk, nc.sbuf_tensor(U("sa_acc"), [128, 2, 1280], F32) as acc,
          nc.sbuf_tensor(U("sa_tmp"), [128, 2, 1280], F32) as tmp,
          nc.sbuf_tensor(U("sa_dt"), [128, 2, 4, 48], F32) as dtt, nc.sbuf_tensor(U("sa_cst"), [128, 3, 48], F32) as cst):
        bw, bx, bacc, btmp, bdt, bcst = Buf(), [[Buf() for _ in range(5)] for _ in range(2)], [Buf(), Buf()], [Buf(), Buf()], [Buf(), Buf()], Buf()
        S.dma("sp", cw[:].rearrange("p a b -> p (a b)"), C.conv_w[l].rearrange("a b -> (a b)").partition_broadcast(128), w=[bw])
        S.dma("sp", cbias[:], C.conv_b[l].partition_broadcast(128), w=[bw])
        S.dma("sp", cst[:, 0, :], C.dt_bias[l].rearrange("a b -> (a b)").partition_broadcast(128), w=[bcst])
        S.dma("sp", cst[:, 1, :], C.a_log[l].rearrange("a b -> (a b)").partition_broadcast(128), w=[bcst])
        S.op("act", lambda e: e.activation(cst[:, 1, :], cst[:, 1, :], AF.Exp), r=[bcst], w=[bcst])
        S.op("dve", lambda e: e.tensor_scalar(cst[:, 1, :], cst[:, 1, :], -1.0, None, ALU.mult), r=[bcst], w=[bcst])
        it = 0
        for t in range(NCH):
            for hf in range(2):
                k2 = it % 2
                it += 1
                c0 = hf * 1280
                for k in range(5):
                    r0 = t * 128 + k - 2
                    lo, hi = max(r0, 0), min(r0 + 128, TL)
                    if lo != r0 or hi != r0 + 128:
                        S.op("pool", lambda e, k2=k2, k=k: e.memset(xk[:, k2, k, :], 0.0), w=[bx[k2][k]])
                    S.dma("sp", xk[lo - r0:hi - r0, k2, k, :], C.xbc_d[lo:hi, c0:c0 + 1280], r=[C.b_proj], w=[bx[k2][k]])
                    cmcol = {(HALF - 1, 4): 0, (HALF - 1, 3): 1, (HALF, 0): 2, (HALF, 1): 3}.get((t, k))
                    if cmcol is not None:
                        S.op("pool", lambda e, k2=k2, k=k, cmcol=cmcol: e.tensor_scalar(xk[:, k2, k, :], xk[:, k2, k, :], C.cm[:, cmcol:cmcol + 1], None, ALU.mult),
                             r=[C.b_fb], w=[bx[k2][k]])
                S.op("dve", lambda e, k2=k2, c0=c0: e.tensor_tensor(acc[:, k2, :], xk[:, k2, 0, :], cw[:, 0, c0:c0 + 1280], ALU.mult), r=[bx[k2][0], bw], w=[bacc[k2]])
                for k in range(1, 5):
                    eng = "pool" if k % 2 == 1 else "dve"
                    S.op(eng, lambda e, k2=k2, k=k, c0=c0: e.tensor_tensor(tmp[:, k2, :], xk[:, k2, k, :], cw[:, k, c0:c0 + 1280], ALU.mult), r=[bx[k2][k], bw], w=[btmp[k2]])
                    S.op("dve", lambda e, k2=k2: e.tensor_tensor(acc[:, k2, :], acc[:, k2, :], tmp[:, k2, :], ALU.add), r=[btmp[k2], bacc[k2]], w=[bacc[k2]])
                S.op("pool", lambda e, k2=k2, c0=c0: e.tensor_tensor(acc[:, k2, :], acc[:, k2, :], cbias[:, c0:c0 + 1280], ALU.add), r=[bw, bacc[k2]], w=[bacc[k2]])
                S.op("act", lambda e, k2=k2: e.activation(acc[:, k2, :], acc[:, k2, :], AF.Silu), r=[bacc[k2]], w=[bacc[k2]])
                S.dma("sp", C.xbc_c[t * 128:(t + 1) * 128, c0:c0 + 1280], acc[:, k2, :], r=[bacc[k2]], w=[C.b_ssd])
            k2 = t % 2
            S.dma("sp", dtt[:, k2, 0, :], C.dt_d[t * 128:(t + 1) * 128, :], r=[C.b_proj], w=[bdt[k2]])
            S.op("dve", lambda e, k2=k2: e.tensor_tensor(dtt[:, k2, 0, :], dtt[:, k2, 0, :], cst[:, 0, :], ALU.add), r=[bcst, bdt[k2]], w=[bdt[k2]])
            S.op("act", lambda e, k2=k2: e.activation(dtt[:, k2, 1, :], dtt[:, k2, 0, :], AF.Abs), r=[bdt[k2]], w=[bdt[k2]])
            S.op("act", lambda e, k2=k2: e.activation(dtt[:, k2, 1, :], dtt[:, k2, 1, :], AF.Exp, scale=-1.0), r=[bdt[k2]], w=[bdt[k2]])
            S.op("dve", lambda e, k2=k2: e.tensor_scalar(dtt[:, k2, 1, :], dtt[:, k2, 1, :], 1.0, None, ALU.add), r=[bdt[k2]], w=[bdt[k2]])
            S.op("act", lambda e, k2=k2: e.activation(dtt[:, k2, 1, :], dtt[:, k2, 1, :], AF.Ln), r=[bdt[k2]], w=[bdt[k2]])
            S.op("dve", lambda e, k2=k2: e.scalar_tensor_tensor(dtt[:, k2, 2, :], dtt[:, k2, 0, :], 0.0, dtt[:, k2, 1, :], ALU.max, ALU.add), r=[bdt[k2]], w=[bdt[k2]])
            S.op("dve", lambda e, k2=k2: e.tensor_tensor(dtt[:, k2, 3, :], dtt[:, k2, 2, :], cst[:, 1, :], ALU.mult), r=[bdt[k2], bcst], w=[bdt[k2]])
            S.dma("sp", C.dtv[t * 128:(t + 1) * 128, :], dtt[:, k2, 2, :], r=[bdt[k2]], w=[C.b_ssd])
            S.dma("sp", C.lav[t * 128:(t + 1) * 128, :], dtt[:, k2, 3, :], r=[bdt[k2]], w=[C.b_ssd])
        S.barrier()
    with contextlib.ExitStack() as es:
        xs = es.enter_context(nc.sbuf_tensor(U("s1_xs"), [128, 1536], F32))
        bcf = es.enter_context(nc.sbuf_tensor(U("s1_bc"), [128, 1024], F32))
        bcb = es.enter_context(nc.sbuf_tensor(U("s1_bcb"), [128, 8, 128], BF16))
        bct = es.enter_context(nc.sbuf_tensor(U("s1_bct"), [128, 8, 128], BF16))
        dl = es.enter_context(nc.sbuf_tensor(U("s1_dl"), [128, 2, 48], F32))
        cbT = es.enter_context(nc.sbuf_tensor(U("s1_cbT"), [128, 4, 128], F32))
        cbm = es.enter_context(nc.sbuf_tensor(U("s1_cbm"), [128, 4, 128], F32))
        xd = es.enter_context(nc.sbuf_tensor(U("s1_xd"), [128, 24, 64], BF16))
        xdw = es.enter_context(nc.sbuf_tensor(U("s1_xdw"), [128, 6, 64], BF16))
        R = es.enter_context(nc.sbuf_tensor(U("s1_R"), [128, 24, 128], F32))
        acol = es.enter_context(nc.sbuf_tensor(U("s1_acol"), [128, 24], F32))
        tmp = es.enter_context(nc.sbuf_tensor(U("s1_tmp"), [128, 2, 128], F32))
        ee = es.enter_context(nc.sbuf_tensor(U("s1_e"), [128, 2, 128], F32))
        M = es.enter_context(nc.sbuf_tensor(U("s1_M"), [128, 2, 128], BF16))
        wd = es.enter_context(nc.sbuf_tensor(U("s1_w"), [128, 3, 24], F32))
        sst = es.enter_context(nc.sbuf_tensor(U("s1_sst"), [128, 1536], F32))
        ysum = es.enter_context(nc.sbuf_tensor(U("s1_ys"), [128, 1536], F32))
        ut = es.enter_context(nc.sbuf_tensor(U("s1_ut"), [128, 2, 128], F32))
        ones = es.enter_context(nc.sbuf_tensor(U("s1_on"), [128, 128], F32))
        pt = es.enter_context(nc.psum_tensor(U("s1_pt"), [128, 8, 128], BF16))
        pcb = es.enter_context(nc.psum_tensor(U("s1_pcb"), [128, 512], F32))
        pacr = es.enter_context(nc.psum_tensor(U("s1_pacr"), [128, 2, 512], F32))
        pyd = es.enter_context(nc.psum_tensor(U("s1_pyd"), [128, 3, 512], F32))
        pst = es.enter_context(nc.psum_tensor(U("s1_pst"), [128, 512], F32))
        b = {k: Buf() for k in ("xs", "bcf", "bcb", "bct", "dl", "cbT", "cbm", "xd", "xdw", "R", "acol", "wd", "sst", "ys", "ut", "pt", "pcb", "pacr", "pyd", "pst")}
        btmp, be, bM = [Buf(), Buf()], [Buf(), Buf()], [Buf(), Buf()]
        S.dma("sp", ut[:], C.utri[:, :, :], w=[b["ut"]])
        S.op("pool", lambda e: e.memset(ones[:], 1.0), w=[b["ut"]])
        hi = 0
        for c in range(NCH):
            S.dma("sp", xs[:], C.xbc_c[c * 128:(c + 1) * 128, 0:1536], r=[C.b_ssd], w=[b["xs"]])
            S.dma("sp", bcf[:], C.xbc_c[c * 128:(c + 1) * 128, 1536:2560], r=[C.b_ssd], w=[b["bcf"]])
            S.dma("sp", dl[:, 0, :], C.dtv[c * 128:(c + 1) * 128, :], r=[C.b_ssd], w=[b["dl"]])
            S.dma("sp", dl[:, 1, :], C.lav[c * 128:(c + 1) * 128, :], r=[C.b_ssd], w=[b["dl"]])
            S.op("act", lambda e: e.copy(bcb[:].rearrange("p a b -> p (a b)"), bcf[:]), r=[b["bcf"]], w=[b["bcb"]])
            for j in range(8):
                S.op("pe", lambda e, j=j: e.transpose(pt[:, j, :], bcb[:, j, :], C.ident[:]), r=[b["bcb"], C.b_ident], w=[b["pt"]])
            S.op("dve", lambda e: e.tensor_copy(bct[:], pt[:]), r=[b["pt"]], w=[b["bct"]])
            S.dma("sp", C.CTs[c], bct[:, 4:8, :], r=[b["bct"]], w=[C.b_ssd2])
            for g in range(4):
                S.op("pe", lambda e, g=g: e.matmul(pcb[:, g * 128:(g + 1) * 128], bct[:, g, :], bct[:, 4 + g, :], start=True, stop=True), r=[b["bct"]], w=[b["pcb"]])
            S.op("act", lambda e: e.copy(cbT[:].rearrange("p a b -> p (a b)"), pcb[:]), r=[b["pcb"]], w=[b["cbT"]])
            for d in range(2):
                lidx = 127 if d == 0 else 0
                S.op("pool", lambda e, d=d: e.tensor_tensor(cbm[:], cbT[:], bc_grp(ut[:, d, :], 4), ALU.mult), r=[b["cbT"], b["ut"]], w=[b["cbm"]])
                S.op("dve", lambda e, d=d: e.tensor_tensor(xd[:], xs[:].rearrange("p (h x) -> p h x", h=24), bc_mid(dl[:, 0, d * 24:(d + 1) * 24], 64), ALU.mult),
                     r=[b["xs"], b["dl"]], w=[b["xd"]])
                S.op("pe", lambda e, d=d: e.matmul(pst[:, 384:408], ut[:, d, :], dl[:, 1, d * 24:(d + 1) * 24], start=True, stop=True), r=[b["ut"], b["dl"]], w=[b["pst"]])
                S.op("dve", lambda e: e.tensor_copy(acol[:], pst[:, 384:408]), r=[b["pst"]], w=[b["acol"]])
                S.op("pool", lambda e, d=d: e.tensor_tensor(R[:], bc_grp(ut[:, d, :], 24), bc_mid(dl[:, 1, d * 24:(d + 1) * 24], 128), ALU.mult),
                     r=[b["ut"], b["dl"]], w=[b["R"]])
                S.op("act", lambda e: e.activation(wd[:, 0, :], acol[:], AF.Exp), r=[b["acol"]], w=[b["wd"]])
                S.dma("sp", C.eacs[d, c], wd[:, 0, :], r=[b["wd"]], w=[C.b_ssd2])
                for g in range(4):
                    S.op("pe", lambda e, g=g: e.matmul(pacr[:, 0, :], ones[:], R[:, g * 6:g * 6 + 4, :], start=True, stop=True), r=[b["R"], b["ut"]], w=[b["pacr"]])
                    S.op("pe", lambda e, g=g: e.matmul(pacr[:, 1, 0:256], ones[:], R[:, g * 6 + 4:g * 6 + 6, :], start=True, stop=True), r=[b["R"], b["ut"]], w=[b["pacr"]])
                    for hh in range(6):
                        h = g * 6 + hh
                        k2 = hi % 2
                        hi += 1
                        src = pacr[:, 0, hh * 128:(hh + 1) * 128] if hh < 4 else pacr[:, 1, (hh - 4) * 128:(hh - 3) * 128]
                        S.op("dve", lambda e, k2=k2, h=h, src=src: e.tensor_scalar(tmp[:, k2, :], src, acol[:, h:h + 1], 0.0, ALU.subtract, ALU.min),
                             r=[b["pacr"], b["acol"]], w=[btmp[k2]])
                        S.op("act", lambda e, k2=k2: e.activation(ee[:, k2, :], tmp[:, k2, :], AF.Exp), r=[btmp[k2]], w=[be[k2]])
                        S.op("pool", lambda e, k2=k2, g=g: e.tensor_tensor(M[:, k2, :], ee[:, k2, :], cbm[:, g, :], ALU.mult), r=[be[k2], b["cbm"]], w=[bM[k2]])
                        S.op("pe", lambda e, k2=k2, h=h: e.matmul(pyd[:, h // 8, (h % 8) * 64:(h % 8 + 1) * 64], M[:, k2, :], xd[:, h, :], start=True, stop=True),
                             r=[bM[k2], b["xd"]], w=[b["pyd"]])
                    t0v = pacr[:, 0, :].rearrange("p (h x) -> p h x", h=4)[:, :, lidx]
                    t1v = pacr[:, 1, 0:256].rearrange("p (h x) -> p h x", h=2)[:, :, lidx]
                    S.op("dve", lambda e, g=g, t0v=t0v: e.tensor_copy(wd[:, 1, g * 6:g * 6 + 4], t0v), r=[b["pacr"]], w=[b["wd"]])
                    S.op("dve", lambda e, g=g, t1v=t1v: e.tensor_copy(wd[:, 1, g * 6 + 4:g * 6 + 6], t1v), r=[b["pacr"]], w=[b["wd"]])
                    S.op("dve", lambda e, g=g: e.tensor_tensor(wd[:, 2, g * 6:g * 6 + 6], wd[:, 1, g * 6:g * 6 + 6], acol[:, g * 6:g * 6 + 6], ALU.subtract),
                         r=[b["wd"], b["acol"]], w=[b["wd"]])
                    S.op("act", lambda e, g=g: e.activation(wd[:, 2, g * 6:g * 6 + 6], wd[:, 2, g * 6:g * 6 + 6], AF.Exp), r=[b["wd"]], w=[b["wd"]])
                    S.op("dve", lambda e, g=g: e.tensor_tensor(xdw[:], xd[:, g * 6:g * 6 + 6, :], bc_mid(wd[:, 2, g * 6:g * 6 + 6], 64), ALU.mult),
                         r=[b["wd"], b["xd"]], w=[b["xdw"]])
                    S.op("pe", lambda e, g=g: e.matmul(pst[:, 0:384], bcb[:, g, :], xdw[:].rearrange("p a b -> p (a b)"), start=True, stop=True),
                         r=[b["bcb"], b["xdw"]], w=[b["pst"]])
                    S.op("act", lambda e, g=g: e.copy(sst[:, g * 384:(g + 1) * 384], pst[:, 0:384]), r=[b["pst"]], w=[b["sst"]])
                S.op("act", lambda e: e.activation(wd[:, 1, :], wd[:, 1, :], AF.Exp), r=[b["wd"]], w=[b["wd"]])
                S.dma("sp", C.cdecs[d, c], wd[:, 1, :], r=[b["wd"]], w=[C.b_ssd2])
                S.dma("sp", C.Sst[d, c], sst[:], r=[b["sst"]], w=[C.b_ssd2])
                pydv = pyd[:].rearrange("p a b -> p (a b)")
                if d == 0:
                    S.op("dve", lambda e: e.tensor_copy(ysum[:], pydv), r=[b["pyd"]], w=[b["ys"]])
                else:
                    S.op("dve", lambda e: e.tensor_tensor(ysum[:], ysum[:], pydv, ALU.add), r=[b["pyd"], b["ys"]], w=[b["ys"]])
            S.dma("sp", C.yacc[c * 128:(c + 1) * 128, :], ysum[:], r=[b["ys"]], w=[C.b_ssd2])
        S.barrier()
    with (nc.sbuf_tensor(U("s2_H"), [128, 24, 64], F32) as H, nc.sbuf_tensor(U("s2_Hb"), [128, 1536], BF16) as Hb,
          nc.sbuf_tensor(U("s2_ct"), [128, 2, 4, 128], BF16) as ct, nc.sbuf_tensor(U("s2_sm"), [128, 2, 2, 24], F32) as sm,
          nc.sbuf_tensor(U("s2_ya"), [128, 2, 1536], F32) as ya, nc.sbuf_tensor(U("s2_sc"), [128, 2, 1536], F32) as sc,
          nc.sbuf_tensor(U("s2_t1"), [128, 24, 64], F32) as t1,
          nc.psum_tensor(U("s2_pyo"), [128, 4, 512], F32) as pyo):
        bH, bHb, bt1, bpyo = Buf(), Buf(), Buf(), Buf()
        bld = [Buf(), Buf()]
        n = 0
        for d in range(2):
            S.op("pool", lambda e: e.memset(H[:], 0.0), w=[bH])
            order = range(NCH) if d == 0 else range(NCH - 1, -1, -1)
            for c in order:
                k2 = n % 2
                n += 1
                S.dma("sp", ct[:, k2, :, :], C.CTs[c], r=[C.b_ssd2], w=[bld[k2]])
                S.dma("sp", sm[:, k2, 0, :], C.eacs[d, c], r=[C.b_ssd2], w=[bld[k2]])
                S.dma("sp", sm[:, k2, 1, :], C.cdecs[d, c], r=[C.b_ssd2], w=[bld[k2]])
                S.dma("sp", ya[:, k2, :], C.yacc[c * 128:(c + 1) * 128, :], r=[C.b_ssd2, C.b_yacc[c]], w=[bld[k2]])
                S.dma("sp", sc[:, k2, :], C.Sst[d, c], r=[C.b_ssd2], w=[bld[k2]])
                if (d == 0 and c == HALF) or (d == 1 and c == HALF - 1):
                    S.op("dve", lambda e: e.tensor_scalar(H[:], H[:], C.fb[:, 1:2], None, ALU.mult), r=[C.b_fb, bH], w=[bH])
                S.op("act", lambda e: e.copy(Hb[:], H[:].rearrange("p a b -> p (a b)")), r=[bH], w=[bHb])
                for g in range(4):
                    S.op("pe", lambda e, g=g, k2=k2: e.matmul(pyo[:, g, 0:384], ct[:, k2, g, :], Hb[:, g * 384:(g + 1) * 384], start=True, stop=True),
                         r=[bld[k2], bHb], w=[bpyo])
                for g in range(4):
                    S.op("dve", lambda e, g=g, k2=k2: e.tensor_tensor(t1[:, g * 6:(g + 1) * 6, :], pyo[:, g, 0:384].rearrange("p (h x) -> p h x", h=6),
                                                                  bc_mid(sm[:, k2, 0, g * 6:(g + 1) * 6], 64), ALU.mult), r=[bpyo, bld[k2]], w=[bt1])
                S.op("pool", lambda e, k2=k2: e.tensor_tensor(ya[:, k2, :], ya[:, k2, :], t1[:].rearrange("p a b -> p (a b)"), ALU.add), r=[bt1, bld[k2]], w=[bld[k2]])
                S.dma("sp", C.yacc[c * 128:(c + 1) * 128, :], ya[:, k2, :], r=[bld[k2]], w=[C.b_yacc[c]])
                S.op("dve", lambda e, k2=k2: e.tensor_tensor(H[:], H[:], bc_mid(sm[:, k2, 1, :], 64), ALU.mult), r=[bH, bld[k2]], w=[bH])
                S.op("pool", lambda e, k2=k2: e.tensor_tensor(H[:], H[:], sc[:, k2, :].rearrange("p (h x) -> p h x", h=24), ALU.add), r=[bH, bld[k2]], w=[bH])
        S.barrier()
    with (nc.sbuf_tensor(U("s3_g"), [128, 1536], F32) as gt, nc.sbuf_tensor(U("s3_dk"), [128, 24], F32) as dk,
          nc.sbuf_tensor(U("s3_y"), [128, 2, 1536], F32) as yt, nc.sbuf_tensor(U("s3_x"), [128, 2, 1536], F32) as xt,
          nc.sbuf_tensor(U("s3_z"), [128, 2, 1536], F32) as zt, nc.sbuf_tensor(U("s3_o"), [128, 2, 1536], BF16) as ot,
          nc.sbuf_tensor(U("s3_sm"), [128, 2, 4], F32) as sm):
        bg = Buf()
        bl, bo = [Buf(), Buf()], [Buf(), Buf()]
        S.dma("sp", gt[:], C.ssd_norm_g[l].partition_broadcast(128), w=[bg])
        S.dma("sp", dk[:], C.d_skip[l].partition_broadcast(128), w=[bg])
        for t in range(NCH):
            k2 = t % 2
            S.dma("sp", yt[:, k2, :], C.yacc[t * 128:(t + 1) * 128, :], r=[C.b_yacc[t]], w=[bl[k2]])
            S.dma("sp", xt[:, k2, :], C.xbc_c[t * 128:(t + 1) * 128, 0:1536], r=[C.b_ssd], w=[bl[k2]])
            S.dma("sp", zt[:, k2, :], C.z_d[t * 128:(t + 1) * 128, :], r=[C.b_proj], w=[bl[k2]])
            S.op("pool", lambda e, k2=k2: e.tensor_tensor(xt[:, k2, :].rearrange("p (h x) -> p h x", h=24), xt[:, k2, :].rearrange("p (h x) -> p h x", h=24),
                                                        bc_mid(dk[:, :], 64), ALU.mult), r=[bl[k2], bg], w=[bl[k2]])
            S.op("dve", lambda e, k2=k2: e.tensor_tensor(yt[:, k2, :], yt[:, k2, :], xt[:, k2, :], ALU.add), r=[bl[k2]], w=[bl[k2]])
            S.op("act", lambda e, k2=k2: e.activation(zt[:, k2, :], zt[:, k2, :], AF.Silu), r=[bl[k2]], w=[bl[k2]])
            S.op("dve", lambda e, k2=k2: e.tensor_tensor(yt[:, k2, :], yt[:, k2, :], zt[:, k2, :], ALU.mult), r=[bl[k2]], w=[bl[k2]])
            S.op("pool", lambda e, k2=k2: e.tensor_tensor(xt[:, k2, :], yt[:, k2, :], yt[:, k2, :], ALU.mult), r=[bl[k2]], w=[bl[k2]])
            S.op("dve", lambda e, k2=k2: e.reduce_sum(sm[:, k2, 0:1], xt[:, k2, :], AX.X), r=[bl[k2]], w=[bl[k2]])
            S.op("act", lambda e, k2=k2: e.activation(sm[:, k2, 1:2], sm[:, k2, 0:1], AF.Ln, bias=C.eps_t[:, 1:2], scale=1.0 / 1536), r=[bl[k2], C.b_ident], w=[bl[k2]])
            S.op("act", lambda e, k2=k2: e.activation(sm[:, k2, 2:3], sm[:, k2, 1:2], AF.Exp, scale=-0.5), r=[bl[k2]], w=[bl[k2]])
            S.op("dve", lambda e, k2=k2: e.scalar_tensor_tensor(ot[:, k2, :], yt[:, k2, :], sm[:, k2, 2:3], gt[:], ALU.mult, ALU.mult), r=[bl[k2], bg], w=[bo[k2]])
            S.dma("sp", C.cat[t * 128:(t + 1) * 128, 0:1536], ot[:, k2, :], r=[bo[k2]], w=[C.b_cat])
        S.barrier()


BIG = 30000.0
NPAIR = TL // 256


def cast_copy(C, S, dst, src, rows, b_dst, step):
    for r0 in range(0, rows, step):
        S.dma("pool", dst[r0:r0 + step, :], src[r0:r0 + step, :], w=[b_dst])


def peer_gates(C, S, l):
    nc = C.nc
    with contextlib.ExitStack() as es:
        A = lambda kind, name, shape, dt: es.enter_context(getattr(nc, kind + "_tensor")(U(name), shape, dt))
        kt = A("sbuf", "pg_kt", [128, 16, 128], BF16)
        qs = A("sbuf", "pg_qs", [128, 2, 16, 128], BF16)
        sc = A("sbuf", "pg_sc", [128, 16, 128], F32)
        sc2 = A("sbuf", "pg_sc2", [128, 16, 128], F32)
        v16 = A("sbuf", "pg_v16", [128, 16, 16], F32)
        idxu = A("sbuf", "pg_idxu", [128, 8, 16], mybir.dt.uint32)
        idxf = A("sbuf", "pg_idxf", [128, 128], F32)
        idxT = A("sbuf", "pg_idxT", [128, 128], F32)
        cand = A("sbuf", "pg_cand", [128, 2, 256], F32)
        tv = A("sbuf", "pg_tv", [128, 16], F32)
        cnd2 = A("sbuf", "pg_cnd2", [128, 256], F32)
        sm = A("sbuf", "pg_sm", [128, 8], F32)
        w1 = A("sbuf", "pg_w1", [128, 2, 16], F32)
        s2m = A("sbuf", "pg_s2m", [128, 2, 128], F32)
        E2 = A("sbuf", "pg_E2", [128, 128], F32)
        sF = A("sbuf", "pg_sF", [128, 16, 128], F32)
        Fm = A("sbuf", "pg_F", [128, 8, 16, 128], BF16)
        FT = A("sbuf", "pg_FT", [128, 128, 128], BF16)
        OH = A("sbuf", "pg_OH", [128, 2, 128], BF16)
        gts = A("sbuf", "pg_gts", [128, 128, 128], BF16)
        iot = A("sbuf", "pg_iot", [128, 128], F32)
        psc = A("psum", "pg_psc", [128, 16, 128], F32)
        ptr = A("psum", "pg_ptr", [128, 2, 8, 128], BF16)
        pg = A("psum", "pg_pg", [128, 2, 4, 128], F32)
        b = {k: Buf() for k in ("kt", "sc", "sc2", "v16", "idxu", "idxf", "idxT", "tv", "sm", "E2", "sF", "F", "FT", "gts", "iot", "psc")}
        bq, bcand, bw1, bs2m, bOH, bptr, bpg = [Buf(), Buf()], [Buf(), Buf()], [Buf(), Buf()], [Buf(), Buf()], [Buf(), Buf()], [Buf(), Buf()], [Buf(), Buf()]
        S.dma("pool", kt[:], C.keysT[l].rearrange("h c d k -> d (h c) k"), w=[b["kt"]])
        S.dma("sp", iot[:], C.iota_in[:, :], w=[b["iot"]])
        qv = C.qT.rearrange("(j p) t -> p j t", p=128)
        nhead = 0
        ntok = 0
        for T in range(NCH):
            qk = T % 2
            S.dma("sp", qs[:, qk, :, :], qv[:, :, T * 128:(T + 1) * 128], r=[C.b_qT], w=[bq[qk]])
            for j in range(16):
                S.op("pe", lambda e, j=j, qk=qk: e.matmul(psc[:, j, :], qs[:, qk, j, :], kt[:, j, :], start=True, stop=True), r=[bq[qk], b["kt"]], w=[b["psc"]])
            S.op("act", lambda e: e.copy(sc[:, 0:8, :], psc[:, 0:8, :]), r=[b["psc"]], w=[b["sc"]])
            S.op("dve", lambda e: e.tensor_copy(sc[:, 8:16, :], psc[:, 8:16, :]), r=[b["psc"]], w=[b["sc"]])
            for j in range(16):
                S.op("dve", lambda e, j=j: e.max(v16[:, j, 0:8], sc[:, j, :]), r=[b["sc"]], w=[b["v16"]])
                S.op("dve", lambda e, j=j: e.match_replace(sc2[:, j, :], v16[:, j, 0:8], sc[:, j, :], -1e30), r=[b["sc"], b["v16"]], w=[b["sc2"]])
                S.op("dve", lambda e, j=j: e.max(v16[:, j, 8:16], sc2[:, j, :]), r=[b["sc2"]], w=[b["v16"]])
                if j % 2 == 0:
                    S.op("dve", lambda e, j=j: e.max_index(idxu[:, j // 2, 0:8], v16[:, j, 0:8], sc[:, j, :]), r=[b["sc"], b["v16"]], w=[b["idxu"]])
                    S.op("dve", lambda e, j=j: e.max_index(idxu[:, j // 2, 8:16], v16[:, j, 8:16], sc2[:, j, :]), r=[b["sc2"], b["v16"]], w=[b["idxu"]])
            S.op("dve", lambda e: e.tensor_copy(idxf[:], idxu[:].rearrange("p a b -> p (a b)")), r=[b["idxu"]], w=[b["idxf"]])
            for h in range(8):
                k2 = nhead % 2
                nhead += 1
                v1, v2 = v16[:, 2 * h, :], v16[:, 2 * h + 1, :]
                S.op("pool", lambda e, k2=k2, v1=v1, v2=v2: e.tensor_tensor(cand[:, k2, :].rearrange("p (a c) -> p a c", a=16), bc_mid(v1, 16), bc_grp(v2, 16), ALU.add),
                     r=[b["v16"]], w=[bcand[k2]])
                S.op("dve", lambda e, k2=k2: e.max(tv[:, 0:8], cand[:, k2, :]), r=[bcand[k2]], w=[b["tv"]])
                S.op("dve", lambda e, k2=k2: e.match_replace(cnd2[:], tv[:, 0:8], cand[:, k2, :], -1e30), r=[b["tv"], bcand[k2]], w=[b["E2"]])
                S.op("dve", lambda e, k2=k2: e.max(tv[:, 8:16], cnd2[:]), r=[b["E2"]], w=[b["tv"]])
                S.op("dve", lambda e: e.tensor_scalar(sm[:, 0:1], tv[:, 0:1], -1.0, None, ALU.mult), r=[b["tv"]], w=[b["sm"]])
                S.op("act", lambda e: e.activation(sF[:, 0, 0:16], tv[:, :], AF.Exp, bias=sm[:, 0:1], scale=1.0, accum_out=sm[:, 1:2]), r=[b["tv"], b["sm"]], w=[b["sm"], b["sF"]])
                S.op("dve", lambda e: e.reciprocal(sm[:, 2:3], sm[:, 1:2]), r=[b["sm"]], w=[b["sm"]])
                S.op("dve", lambda e, v1=v1: e.tensor_scalar(sm[:, 3:4], v1[:, 0:1], -1.0, None, ALU.mult), r=[b["v16"]], w=[b["sm"]])
                S.op("dve", lambda e, v2=v2: e.tensor_scalar(sm[:, 4:5], v2[:, 0:1], -1.0, None, ALU.mult), r=[b["v16"]], w=[b["sm"]])
                S.op("act", lambda e, k2=k2, v1=v1: e.activation(w1[:, k2, :], v1, AF.Exp, bias=sm[:, 3:4], scale=1.0), r=[b["v16"], b["sm"]], w=[bw1[k2]])
                S.op("dve", lambda e, k2=k2: e.tensor_scalar(w1[:, k2, :], w1[:, k2, :], sm[:, 2:3], None, ALU.mult), r=[b["sm"], bw1[k2]], w=[bw1[k2]])
                S.op("pool", lambda e, k2=k2, h=h, v2=v2: e.tensor_scalar(s2m[:, k2, :], sc[:, 2 * h + 1, :], v2[:, 15:16], BIG, ALU.is_ge, ALU.mult), r=[b["sc"], b["v16"]], w=[bs2m[k2]])
                S.op("dve", lambda e, k2=k2, h=h: e.scalar_tensor_tensor(s2m[:, k2, :], s2m[:, k2, :], -BIG, sc[:, 2 * h + 1, :], ALU.add, ALU.add), r=[b["sc"], bs2m[k2]], w=[bs2m[k2]])
                S.op("act", lambda e, k2=k2: e.activation(E2[:], s2m[:, k2, :], AF.Exp, bias=sm[:, 4:5], scale=1.0), r=[bs2m[k2], b["sm"]], w=[b["E2"]])
                S.op("pool", lambda e, k2=k2, v1=v1: e.tensor_tensor(sF[:], bc_grp(s2m[:, k2, :], 16), bc_mid(v1, 128), ALU.add), r=[bs2m[k2], b["v16"]], w=[b["sF"]])
                S.op("dve", lambda e: e.scalar_tensor_tensor(sF[:], sF[:], tv[:, 15:16], bc_grp(E2[:, :], 16), ALU.is_ge, ALU.mult), r=[b["sF"], b["tv"], b["E2"]], w=[b["sF"]])
                S.op("dve", lambda e, k2=k2, h=h: e.tensor_tensor(Fm[:, h, :, :], sF[:], bc_mid(w1[:, k2, :], 128), ALU.mult), r=[b["sF"], bw1[k2]], w=[b["F"]])
            Fv = Fm[:].rearrange("p h a i -> p (h a) i")
            for g in range(16):
                pk = g % 2
                for j in range(8):
                    i2 = g * 8 + j
                    S.op("pe", lambda e, pk=pk, j=j, i2=i2: e.transpose(ptr[:, pk, j, :], Fv[:, :, i2], C.ident[:]), r=[b["F"], C.b_ident], w=[bptr[pk]])
                if g % 2 == 0:
                    S.op("act", lambda e, pk=pk, g=g: e.copy(FT[:, g * 8:(g + 1) * 8, :], ptr[:, pk, :, :]), r=[bptr[pk]], w=[b["FT"]])
                else:
                    S.op("dve", lambda e, pk=pk, g=g: e.tensor_copy(FT[:, g * 8:(g + 1) * 8, :], ptr[:, pk, :, :]), r=[bptr[pk]], w=[b["FT"]])
            S.op("pe", lambda e: e.transpose(pg[:, 0, 0, :], idxf[:], C.ident32[:]), r=[b["idxf"], C.b_ident, bpg[0]], w=[bpg[0]])
            S.op("dve", lambda e: e.tensor_copy(idxT[:], pg[:, 0, 0, :]), r=[bpg[0]], w=[b["idxT"]])
            for t4 in range(32):
                pk = t4 % 2
                for tt in range(4):
                    t = t4 * 4 + tt
                    ok = ntok % 2
                    ntok += 1
                    S.op("dve", lambda e, ok=ok, t=t: e.tensor_scalar(OH[:, ok, :], iot[:], idxT[:, t:t + 1], None, ALU.is_equal), r=[b["iot"], b["idxT"]], w=[bOH[ok]])
                    S.op("pe", lambda e, ok=ok, t=t, pk=pk, tt=tt: e.matmul(pg[:, pk, tt, :], FT[:, :, t], OH[:, ok, :], start=True, stop=True), r=[b["FT"], bOH[ok]], w=[bpg[pk]])
                dst = gts[:, :, t4 * 4:(t4 + 1) * 4].rearrange("p i t -> p t i")
                if t4 % 2 == 0:
                    S.op("act", lambda e, pk=pk, dst=dst: e.copy(dst, pg[:, pk, :, :]), r=[bpg[pk]], w=[b["gts"]])
                else:
                    S.op("dve", lambda e, pk=pk, dst=dst: e.tensor_copy(dst, pg[:, pk, :, :]), r=[bpg[pk]], w=[b["gts"]])
            gd = C.GTd[T // 4]
            for q in range(4):
                S.dma("sp", gd[:, q * 32:(q + 1) * 32, (T % 4) * 128:(T % 4) * 128 + 128], gts[:, q * 32:(q + 1) * 32, :], r=[b["gts"]], w=[C.b_GT])
        S.barrier()


def peer_dense(C, S, l):
    nc = C.nc
    NQ = TL // 512
    with (nc.sbuf_tensor(U("pd_hs"), [128, 32, 512], BF16) as hs, nc.sbuf_tensor(U("pd_ae"), [128, 16, 512], BF16) as ae,
          nc.sbuf_tensor(U("pd_gt"), [128, 16, 512], BF16) as gte,
          nc.sbuf_tensor(U("pd_u"), [128, 2, 32, 128], BF16) as ub, nc.sbuf_tensor(U("pd_v"), [128, 2, 16, 512], BF16) as vb,
          nc.sbuf_tensor(U("pd_gl"), [128, 2, 512], BF16) as gl, nc.sbuf_tensor(U("pd_of"), [128, 2, 512], F32) as of,
          nc.sbuf_tensor(U("pd_os"), [128, 4, D], F32) as osb,
          nc.psum_tensor(U("pd_pS"), [128, 2, 512], F32) as pS, nc.psum_tensor(U("pd_pV"), [128, 4, 512], F32) as pV):
        bh, bae, bgt, bos = Buf(), Buf(), Buf(), Buf()
        bu, bv, bgl, bpS, bpV, bof = [Buf(), Buf()], [Buf(), Buf()], [Buf(), Buf()], [Buf(), Buf()], [Buf() for _ in range(4)], [Buf(), Buf()]
        hv = C.hT.rearrange("(c p) t -> p c t", p=128)
        vv = C.vb.rearrange("(i p) d -> p i d", p=128)
        nu = nv = ns = npv = nof = 0
        for qd in range(NQ):
            S.dma("sp", hs[:, 0:16, :], hv[:, 0:16, qd * 512:(qd + 1) * 512], r=[C.b_hT], w=[bh])
            S.dma("sp", hs[:, 16:32, :], hv[:, 16:32, qd * 512:(qd + 1) * 512], r=[C.b_hT], w=[bh])
            for eg in range(8):
                S.dma("sp", gte[:], C.GTd[qd][:, eg * 16:(eg + 1) * 16, :], r=[C.b_GT], w=[bgt])
                for ci in range(16):
                    i1 = eg * 16 + ci
                    uk = nu % 2
                    nu += 1
                    S.dma("sp", ub[:, uk, :, :], C.uTb[i1], r=[C.b_uv], w=[bu[uk]])
                    sk = ns % 2
                    ns += 1
                    for dc in range(32):
                        S.op("pe", lambda e, sk=sk, dc=dc, uk=uk: e.matmul(pS[:, sk, :], ub[:, uk, dc, :], hs[:, dc, :], start=(dc == 0), stop=(dc == 31)),
                             r=[bu[uk], bh], w=[bpS[sk]])
                    S.op("act", lambda e, sk=sk: e.activation(gl[:, sk, :], pS[:, sk, :], AF.Gelu), r=[bpS[sk]], w=[bgl[sk]])
                    S.op("dve", lambda e, sk=sk, ci=ci: e.tensor_tensor(ae[:, ci, :], gte[:, ci, :], gl[:, sk, :], ALU.mult), r=[bgl[sk], bgt], w=[bae])
                for db in range(8):
                    vk = nv % 2
                    nv += 1
                    S.dma("sp", vb[:, vk, :, :], vv[:, eg * 16:(eg + 1) * 16, db * 512:(db + 1) * 512], r=[C.b_uv], w=[bv[vk]])
                    for tt in range(4):
                        pk = npv % 4
                        npv += 1
                        for ci in range(16):
                            S.op("pe", lambda e, vk=vk, ci=ci, tt=tt, pk=pk: e.matmul(
                                pV[:, pk, :], ae[:, ci, tt * 128:(tt + 1) * 128], vb[:, vk, ci, :], start=(ci == 0), stop=(ci == 15)),
                                r=[bae, bv[vk]], w=[bpV[pk]])
                        dst = osb[:, tt, db * 512:(db + 1) * 512]
                        if eg == 0:
                            S.op("act", lambda e, pk=pk, dst=dst: e.copy(dst, pV[:, pk, :]), r=[bpV[pk]], w=[bos])
                        else:
                            S.op("dve", lambda e, pk=pk, dst=dst: e.tensor_tensor(dst, dst, pV[:, pk, :], ALU.add), r=[bpV[pk], bos], w=[bos])
            for tt in range(4):
                for db in range(8):
                    k = nof % 2
                    nof += 1
                    t0 = qd * 512 + tt * 128
                    S.dma("sp", of[:, k, :], C.hres[t0:t0 + 128, db * 512:(db + 1) * 512], r=[C.b_hres], w=[bof[k]])
                    S.op("pool", lambda e, k=k: e.tensor_scalar(of[:, k, :], of[:, k, :], ALPHA, None, ALU.mult), r=[bof[k]], w=[bof[k]])
                    S.op("pool", lambda e, k=k, tt=tt, db=db: e.tensor_tensor(of[:, k, :], of[:, k, :], osb[:, tt, db * 512:(db + 1) * 512], ALU.add), r=[bof[k], bos], w=[bof[k]])
                    S.dma("sp", C.pre[t0:t0 + 128, db * 512:(db + 1) * 512], of[:, k, :], r=[bof[k]], w=[C.b_pre])
        S.barrier()


def cast_u(C, S, l):
    src = C.peer_uT[l].rearrange("(c p) (i e) -> p c i e", p=128, e=128)
    dst = C.uTb.rearrange("i p c e -> p c i e")
    for c in range(32):
        for ig in range(8):
            S.dma("pool", dst[:, c, ig * 16:(ig + 1) * 16, :], src[:, c, ig * 16:(ig + 1) * 16, :], w=[C.b_uv])


def build(dbg=(), stop_after=None, layers=DEPTH):
    nc = bass.Bass("TRN2", target_bir_lowering=False)
    C = Ctx()
    C.nc = nc
    S = Sched(nc)
    inp = lambda name, shape, dt=F32: nc.dram_tensor(name, list(shape), dt, kind="ExternalInput").ap()
    C.x = inp("x_loc", [TL, D])
    C.fb_in = inp("fb", [128, 4])
    C.ln_in_g = inp("ln_in_g", [D]); C.ln_in_b = inp("ln_in_b", [D])
    C.w_in = inp("w_in", [DEPTH, D, PW])
    C.ident_in = inp("ident", [128, 128])
    C.tdist = inp("tdist", [128, 127, 128]); C.tdw = inp("tdw", [128, 17, 128]); C.tlnm = inp("tlnm", [128, 17, 128])
    C.diff_lambda = inp("diff_lambda", [DEPTH, 4, 64]); C.diff_norm_g = inp("diff_norm_g", [DEPTH, 128])
    C.conv_w = inp("conv_w", [DEPTH, 5, 2560]); C.conv_b = inp("conv_b", [DEPTH, 2560])
    C.dt_bias = inp("dt_bias", [DEPTH, 2, 24]); C.a_log = inp("a_log", [DEPTH, 2, 24]); C.d_skip = inp("d_skip", [DEPTH, 24])
    C.ssd_norm_g = inp("ssd_norm_g", [DEPTH, 1536])
    C.utri = inp("utri", [128, 2, 128]); C.cm_in = inp("cm", [128, 4])
    C.w_out = inp("w_out", [DEPTH, D, D]); C.ln1_g = inp("ln1_g", [DEPTH, D]); C.ln1_b = inp("ln1_b", [DEPTH, D])
    C.peer_wq = inp("peer_wq", [DEPTH, D, 2048]); C.keysT = inp("peer_keysT", [DEPTH, 8, 2, 128, 128])
    C.peer_uT = inp("peer_uT", [DEPTH, D, 16384]); C.peer_v = inp("peer_v", [DEPTH, 16384, D])
    C.ln2_g = inp("ln2_g", [DEPTH, D]); C.ln2_b = inp("ln2_b", [DEPTH, D]); C.iota_in = inp("iota", [128, 128])
    C.y = nc.dram_tensor("y_loc", [TL, D], F32, kind="ExternalOutput").ap()
    C.hres = _dram(nc, dbg, "hres", [TL, D], F32); C.b_hres = Buf()
    C.hT = _dram(nc, dbg, "hT", [D, TL], BF16); C.b_hT = Buf()
    C.z_d = _dram(nc, dbg, "z_d", [TL, 1536], F32)
    C.xbc_d = _dram(nc, dbg, "xbc_d", [TL, 2560], F32)
    C.dt_d = _dram(nc, dbg, "dt_d", [TL, 48], F32)
    C.QTd = _dram(nc, dbg, "QTd", [1024, TL], BF16)
    C.QTl = _dram(nc, dbg, "QTl", [1536, TL], BF16)
    C.KTd = _dram(nc, dbg, "KTd", [1024, TL], BF16)
    C.KTl = _dram(nc, dbg, "KTl", [1536, TL], BF16)
    C.Vd = _dram(nc, dbg, "Vd", [TL, 8, 129], BF16)
    C.Vl = _dram(nc, dbg, "Vl", [TL, 12, 129], BF16)
    C.cat = _dram(nc, dbg, "cat", [TL, D], BF16); C.b_cat = Buf()
    C.xbc_c = _dram(nc, dbg, "xbc_c", [TL, 2560], F32)
    C.dtv = _dram(nc, dbg, "dtv", [TL, 48], F32); C.lav = _dram(nc, dbg, "lav", [TL, 48], F32)
    C.CTs = _dram(nc, dbg, "CTs", [NCH, 128, 4, 128], BF16)
    C.Sst = _dram(nc, dbg, "Sst", [2, NCH, 128, 1536], F32)
    C.eacs = _dram(nc, dbg, "eacs", [2, NCH, 128, 24], F32); C.cdecs = _dram(nc, dbg, "cdecs", [2, NCH, 128, 24], F32)
    C.yacc = _dram(nc, dbg, "yacc", [TL, 1536], F32)
    C.b_ssd, C.b_ssd2 = Buf(), Buf()
    C.catT = _dram(nc, dbg, "catT", [D, TL], BF16); C.b_catT = Buf()
    C.pre = _dram(nc, dbg, "pre", [TL, D], F32); C.b_pre = Buf()
    C.qT = _dram(nc, dbg, "qT", [2048, TL], BF16); C.b_qT = Buf()
    C.w_in_b = _dram(nc, dbg, "w_in_b", [D, PW], BF16); C.w_out_b = _dram(nc, dbg, "w_out_b", [D, D], BF16); C.b_wb = Buf()
    C.GTd = _dram(nc, dbg, "GTd", [TL // 512, 128, 128, 512], BF16); C.b_GT = Buf()
    C.uTb = _dram(nc, dbg, "uTb", [128, 128, 32, 128], BF16); C.vb = _dram(nc, dbg, "vb", [16384, D], BF16); C.b_uv = Buf()
    C.b_yacc = [Buf() for _ in range(NCH)]
    C.b_proj = Buf()
    C.b_ident0 = Buf()
    C.b_y = Buf()
    with (nc.sbuf_tensor(U("ident_bf"), [128, 128], BF16) as ident, nc.sbuf_tensor(U("eps_t"), [128, 2], F32) as eps_t,
          nc.sbuf_tensor(U("fb_t"), [128, 4], F32) as fb, nc.sbuf_tensor(U("gn_t"), [128, 128], F32) as gn,
          nc.sbuf_tensor(U("nlam_t"), [128, 2], F32) as nlam, nc.sbuf_tensor(U("cm_t"), [128, 4], F32) as cm,
          nc.sbuf_tensor(U("ident32"), [128, 128], F32) as ident32):
        C.ident32 = ident32
        S.dma("sp", ident32[:], C.ident_in[:, :], w=[C.b_ident0])
        C.ident, C.eps_t, C.fb, C.gn, C.nlam, C.cm = ident, eps_t, fb, gn, nlam, cm
        C.b_ident, C.b_fb, C.b_gn = Buf(), Buf(), Buf()
        S.dma("sp", cm[:], C.cm_in[:, :], w=[C.b_fb])
        S.dma("pool", ident[:], C.ident_in[:, :], w=[C.b_ident])
        S.dma("sp", fb[:], C.fb_in[:, :], w=[C.b_fb])
        S.op("pool", lambda e: e.memset(C.eps_t[:, 0:1], LN_EPS), w=[C.b_ident])
        S.op("pool", lambda e: e.memset(C.eps_t[:, 1:2], 1e-5), w=[C.b_ident])
        S.barrier()

        def src_x(t, xt, bx):
            S.dma("sp", xt[:], C.x[t * 128:(t + 1) * 128, :], w=[bx])
        ln_phase(C, S, src_x, C.ln_in_g, C.ln_in_b)
        for l in range(layers):
            if stop_after == "ln0":
                break
            cast_copy(C, S, C.w_in_b, C.w_in[l], D, C.b_wb, 256)
            cast_copy(C, S, C.w_out_b, C.w_out[l], D, C.b_wb, 512)
            S.barrier()
            bA, bB = proj_blocks(C, l)
            gemm_phase(C, S, C.hT, C.b_hT, C.w_in_b, bA, bB)
            if stop_after == "proj":
                break
            if stop_after != "nossd":
                ssd_phase(C, S, l)
            if stop_after == "ssd":
                break
            lam_prep(C, S, l, 0.8 - 0.6 * float(np.exp(-0.3 * l)))
            attn_phase(C, S, "diff")
            if stop_after == "diff":
                break
            attn_phase(C, S, "dil")
            if stop_after == "dil":
                break
            transpose_phase(C, S, C.cat, C.b_cat, C.catT, C.b_catT)
            bA = [(j * 512, 512, "res", lambda t0, j=j: C.pre[t0:t0 + 128, j * 512:(j + 1) * 512]) for j in range(8)]
            gemm_phase(C, S, C.catT, C.b_catT, C.w_out_b, bA, [])

            def src_pre(t, xt, bx):
                S.dma("sp", xt[:], C.pre[t * 128:(t + 1) * 128, :], r=[C.b_pre], w=[bx])
            ln_phase(C, S, src_pre, C.ln1_g[l], C.ln1_b[l])
            if stop_after == "ln1":
                break
            cast_u(C, S, l)
            cast_copy(C, S, C.vb, C.peer_v[l], 16384, C.b_uv, 1024)
            bB = [(j * 512, 1.0, lambda ch, tb, j=j: C.qT[j * 512 + ch * 128:j * 512 + (ch + 1) * 128, tb:tb + 512]) for j in range(4)]
            C.b_proj_save = C.b_proj
            gemm_phase(C, S, C.hT, C.b_hT, C.peer_wq[l], [], bB)
            peer_gates(C, S, l)
            if stop_after == "gates":
                break
            peer_dense(C, S, l)
            last = l == layers - 1

            def wout(t, xt, bx):
                S.dma("sp", C.y[t * 128:(t + 1) * 128, :], xt[:], r=[bx], w=[C.b_y])
            ln_phase(C, S, src_pre, C.ln2_g[l], C.ln2_b[l], write_out=wout if last else None)
        if stop_after is not None:
            S.dma("sp", C.y[:, :], C.hres[:, :], r=[C.b_hres], w=[C.b_y])
        S.barrier()
    return nc


def host_tables():
    k = np.arange(128)[:, None, None]
    q = np.arange(128)[None, None, :]
    j = np.arange(127)[None, :, None]
    tdist = np.abs((j - 63) * 128 + k - q).astype(np.float32)
    o = np.arange(17)[None, :, None]
    rel = (o - 8) * 128 + k - q
    tdw = np.abs(rel).astype(np.float32)
    mult = np.zeros(rel.shape, np.float64)
    for r in (1, 4, 16):
        mult += ((rel % r) == 0) & (np.abs(rel) <= 64 * r)
    tlnm = np.where(mult > 0, np.log(np.maximum(mult, 1)), -30000.0).astype(np.float32)
    return tdist, np.ascontiguousarray(np.broadcast_to(tdw, (128, 17, 128))), tlnm


def shard_inputs(inputs, names=None):
    xp, xs = inputs["x_prompt"], inputs["x_sample"]
    maps = []
    ident = np.eye(128, dtype=np.float32)
    tdist, tdw, tlnm = host_tables()
    shared = {}
    for c in range(NCORE):
        x_loc = np.concatenate([xp[0], xp[1]], axis=0) if c == 0 else xs[0]
        flag = 1.0 if c == 0 else 0.0
        fb = np.zeros((128, 4), np.float32)
        fb[:, 0] = -30000.0 * flag
        fb[:, 1] = 1.0 - flag
        cm = np.ones((128, 4), np.float32)
        cm[126:128, 0] = 1.0 - flag
        cm[127, 1] = 1.0 - flag
        cm[0:2, 2] = 1.0 - flag
        cm[0, 3] = 1.0 - flag
        kk, ll = np.arange(128)[:, None], np.arange(128)[None, :]
        utri = np.stack([(kk <= ll), (kk >= ll)], axis=1).astype(np.float32)
        m = {"x_loc": np.ascontiguousarray(x_loc), "ident": ident, "fb": fb, "tdist": tdist, "tdw": tdw, "tlnm": tlnm, "cm": cm, "utri": np.ascontiguousarray(utri)}
        m["iota"] = np.ascontiguousarray(np.broadcast_to(np.arange(128, dtype=np.float32)[None, :], (128, 128)))
        for k in (names or WEIGHT_NAMES):
            if k == "peer_keysT":
                m[k] = shared.setdefault(k, np.ascontiguousarray(np.transpose(inputs["peer_keys"], (0, 1, 2, 4, 3)), dtype=np.float32))
            elif k == "peer_uT":
                m[k] = shared.setdefault(k, np.ascontiguousarray(np.transpose(inputs["peer_u"], (0, 2, 1)), dtype=np.float32))
            else:
                m[k] = shared.setdefault(k, np.ascontiguousarray(inputs[k], dtype=np.float32))
        maps.append(m)
    return maps


WEIGHT_NAMES = ("ln_in_g", "ln_in_b", "w_in", "diff_lambda", "diff_norm_g", "conv_w", "conv_b", "dt_bias", "a_log", "d_skip", "ssd_norm_g", "w_out", "ln1_g", "ln1_b", "peer_wq", "peer_keysT", "peer_uT", "peer_v", "ln2_g", "ln2_b")


def gather_outputs(results):
    y0 = results[0]["y_loc"]
    y1 = results[1]["y_loc"]
    return np.ascontiguousarray(y0.reshape(2, 4096, D)), np.ascontiguousarray(y1.reshape(1, 8192, D))


def kernel(**inputs):
    nc = build()
    res = run_bass_kernel_spmd(nc, shard_inputs(inputs), core_ids=list(range(NCORE)))
    return gather_outputs(res.results)
```
